# Optimizing a Trainium2 kernel written in Bass

```python
import jax, jax.numpy as jnp
from jax import lax
import numpy as np

D_MODEL = 2048
BATCH = 1
SEQ = 16384
DEPTH = 2

N_A_LAYERS = DEPTH // 2
N_B_LAYERS = DEPTH - N_A_LAYERS
EPS = 1e-6

RET_HEADS = 8
RET_DK = D_MODEL // RET_HEADS
RET_DV = 2 * RET_DK
RET_CHUNK = 128
RET_THETA = 10000.0
RET_IN = 2 * RET_HEADS * RET_DK + 2 * RET_HEADS * RET_DV

DIL_GROUPS = ((128, 1), (512, 4), (2048, 16))
N_GROUPS = len(DIL_GROUPS)
HEAD_DIM = 128
Q_HEADS = D_MODEL // HEAD_DIM
KV_HEADS = 4
ROT_DIM = HEAD_DIM // 4
ROPE_THETA = 500000.0
ATTN_BLOCK = 128
Q_COLS = N_GROUPS * Q_HEADS * HEAD_DIM
KV_COLS = N_GROUPS * 2 * KV_HEADS * HEAD_DIM

FFN_HIDDEN = -(-8 * D_MODEL // (3 * 256)) * 256
N_MOD = 6

kernel_name = "yoco_retention_dilated_attn_adaln"


def rmsnorm(x, g):
    xf = x.astype(jnp.float32)
    y = xf * lax.rsqrt(jnp.mean(xf * xf, axis=-1, keepdims=True) + EPS)
    return (y * g.astype(jnp.float32)).astype(x.dtype)


def modulate(h, shift, scale):
    return h * (1 + scale[:, None, :]) + shift[:, None, :]


def apply_rotary(x, positions, inv_freq):
    rot = 2 * inv_freq.shape[0]
    ang = positions.astype(jnp.float32)[..., None] * inv_freq
    ang = ang.reshape(ang.shape[:2] + (1,) * (x.ndim - 3) + ang.shape[-1:])
    cos, sin = jnp.cos(ang), jnp.sin(ang)
    xr = x[..., :rot].astype(jnp.float32)
    x1, x2 = jnp.split(xr, 2, axis=-1)
    out = jnp.concatenate([x1 * cos - x2 * sin, x2 * cos + x1 * sin], axis=-1).astype(x.dtype)
    return jnp.concatenate([out, x[..., rot:]], axis=-1)


def swiglu(h, w_in, w_out):
    gate, up = jnp.split(h @ w_in, 2, axis=-1)
    return (jax.nn.silu(gate) * up) @ w_out


def chunkwise_retention(q, k, v):
    B, S, H, dk = q.shape
    dv = v.shape[-1]
    n = S // RET_CHUNK
    log_g = jnp.log1p(-(2.0 ** (-5.0 - jnp.arange(H, dtype=jnp.float32))))
    idx = jnp.arange(RET_CHUNK, dtype=jnp.float32)
    diff = idx[:, None] - idx[None, :]
    decay_mask = jnp.where(diff >= 0, jnp.exp(log_g[:, None, None] * jnp.maximum(diff, 0.0)), 0.0)
    q_decay = jnp.exp(log_g[:, None] * (idx + 1.0))
    k_decay = jnp.exp(log_g[:, None] * (RET_CHUNK - 1.0 - idx))
    chunk_decay = jnp.exp(log_g * RET_CHUNK)

    def to_chunks(t):
        return t.astype(jnp.float32).reshape(B, n, RET_CHUNK, H, t.shape[-1]).transpose(1, 0, 3, 2, 4)

    def step(R, xs):
        qc, kc, vc = xs
        scores = jnp.einsum('bhid,bhjd->bhij', qc, kc) * decay_mask
        o = (jnp.einsum('bhij,bhje->bhie', scores, vc)
             + jnp.einsum('bhid,bhde->bhie', qc, R) * q_decay[None, :, :, None])
        R = (R * chunk_decay[None, :, None, None]
             + jnp.einsum('bhjd,bhje->bhde', kc * k_decay[None, :, :, None], vc))
        return R, o

    R0 = jnp.zeros((B, H, dk, dv), jnp.float32)
    _, o = lax.scan(step, R0, (to_chunks(q), to_chunks(k), to_chunks(v)))
    return o.transpose(1, 0, 3, 2, 4).reshape(B, S, H, dv)


def retention_mixer(h, positions, w_in, w_out):
    B, S, _ = h.shape
    hk, hv = RET_HEADS * RET_DK, RET_HEADS * RET_DV
    q, k, v, g = jnp.split(h @ w_in, [hk, 2 * hk, 2 * hk + hv], axis=-1)
    inv_freq = 1.0 / (RET_THETA ** jnp.linspace(0.0, 1.0, RET_DK // 2, dtype=jnp.float32))
    q = apply_rotary(q.reshape(B, S, RET_HEADS, RET_DK), positions, inv_freq)
    k = apply_rotary(k.reshape(B, S, RET_HEADS, RET_DK), positions, inv_freq) * (RET_DK ** -0.5)
    v = v.reshape(B, S, RET_HEADS, RET_DV)
    y = chunkwise_retention(q, k, v)
    y = y * lax.rsqrt(jnp.mean(y * y, axis=-1, keepdims=True) + EPS)
    y = y.reshape(B, S, hv).astype(h.dtype)
    return (jax.nn.silu(g) * y) @ w_out


def shared_kv(x, c, positions, kv_norm_g, kv_ada_w, kv_ada_b, kv_w):
    B, S, _ = x.shape
    shift, scale = jnp.split(jax.nn.silu(c) @ kv_ada_w + kv_ada_b, 2, axis=-1)
    h = modulate(rmsnorm(x, kv_norm_g), shift, scale)
    kv = (h @ kv_w).reshape(B, S, N_GROUPS, 2, KV_HEADS, HEAD_DIM)
    inv_freq = ROPE_THETA ** (-jnp.arange(0, ROT_DIM, 2, dtype=jnp.float32) / ROT_DIM)
    k = apply_rotary(kv[:, :, :, 0], positions, inv_freq)
    return k, kv[:, :, :, 1]


def dilated_group_attention(q, k, v, window, dilation):
    B, S, Hq, Dh = q.shape
    Hkv = k.shape[2]
    rep = Hq // Hkv
    span = dilation * ATTN_BLOCK
    s_pad = -(-S // span) * span
    L = s_pad // dilation
    nb = L // ATTN_BLOCK

    def to_strided(t):
        t = jnp.pad(t.astype(jnp.float32), ((0, 0), (0, s_pad - S), (0, 0), (0, 0)))
        t = t.reshape(B, L, dilation, t.shape[2], Dh).transpose(0, 2, 1, 3, 4)
        return t.reshape(B, dilation, nb, ATTN_BLOCK, t.shape[3], Dh)

    def with_prev(t):
        prev = jnp.concatenate([jnp.zeros_like(t[:, :, :1]), t[:, :, :-1]], axis=2)
        return jnp.concatenate([prev, t], axis=3)

    def from_strided(t):
        perm = (0, 2, 3, 1) + tuple(range(4, t.ndim))
        return t.transpose(perm).reshape((B, s_pad) + t.shape[4:])[:, :S]

    qs = to_strided(q).reshape(B, dilation, nb, ATTN_BLOCK, Hkv, rep, Dh)
    kb = with_prev(to_strided(k))
    vb = with_prev(to_strided(v))
    scores = jnp.einsum('bdnqgrh,bdnkgh->bdngrqk', qs, kb) * (Dh ** -0.5)
    i = jnp.arange(ATTN_BLOCK)[:, None]
    j = jnp.arange(2 * ATTN_BLOCK)[None, :]
    rel = i - j + ATTN_BLOCK
    band = (rel >= 0) & (rel <= window // dilation)
    key_idx = jnp.arange(nb)[:, None] * ATTN_BLOCK - ATTN_BLOCK + jnp.arange(2 * ATTN_BLOCK)[None, :]
    mask = band[None] & (key_idx >= 0)[:, None, :]
    scores = jnp.where(mask[None, None, :, None, None], scores, -jnp.inf)
    lse = jax.nn.logsumexp(scores, axis=-1)
    p = jnp.exp(scores - lse[..., None])
    o = jnp.einsum('bdngrqk,bdnkgh->bdnqgrh', p, vb).reshape(B, dilation, nb, ATTN_BLOCK, Hq, Dh)
    lse = lse.transpose(0, 1, 2, 5, 3, 4).reshape(B, dilation, nb, ATTN_BLOCK, Hq)
    return from_strided(o), from_strided(lse)


def dilated_mixer(h, positions, k_sh, v_sh, w_q, w_out):
    B, S, _ = h.shape
    inv_freq = ROPE_THETA ** (-jnp.arange(0, ROT_DIM, 2, dtype=jnp.float32) / ROT_DIM)
    q = apply_rotary((h @ w_q).reshape(B, S, N_GROUPS, Q_HEADS, HEAD_DIM), positions, inv_freq)
    outs, lses = [], []
    for g, (window, dilation) in enumerate(DIL_GROUPS):
        o, lse = dilated_group_attention(q[:, :, g], k_sh[:, :, g], v_sh[:, :, g], window, dilation)
        outs.append(o)
        lses.append(lse)
    weights = jax.nn.softmax(jnp.stack(lses, axis=0), axis=0)
    o = jnp.sum(weights[..., None] * jnp.stack(outs, axis=0), axis=0)
    return o.reshape(B, S, Q_HEADS * HEAD_DIM).astype(h.dtype) @ w_out


def setup_inputs(seed: int = 0) -> dict:
    key = jax.random.key(seed)
    ks = jax.random.split(key, 20)
    f32 = jnp.float32
    D = D_MODEL

    def w(k, shape, fan_in, gain=1.0):
        return jax.random.normal(k, shape, f32) * (gain * fan_in ** -0.5)

    return {
        "x": jax.random.normal(ks[0], (BATCH, SEQ, D), f32),
        "c": jax.random.normal(ks[1], (BATCH, D), f32),
        "positions": (jax.random.randint(ks[2], (BATCH, 1), 0, 1024, jnp.int32)
                      + jnp.arange(SEQ, dtype=jnp.int32)[None, :]),
        "ada_w": w(ks[3], (DEPTH, D, N_MOD * D), D, 0.5),
        "ada_b": 0.02 * jax.random.normal(ks[4], (DEPTH, N_MOD * D), f32),
        "norm_g": 1.0 + 0.02 * jax.random.normal(ks[5], (DEPTH, 2, D), f32),
        "ffn_w_in": w(ks[6], (DEPTH, D, 2 * FFN_HIDDEN), D),
        "ffn_w_out": w(ks[7], (DEPTH, FFN_HIDDEN, D), FFN_HIDDEN),
        "ret_w_in": w(ks[8], (N_A_LAYERS, D, RET_IN), D),
        "ret_w_out": w(ks[9], (N_A_LAYERS, RET_HEADS * RET_DV, D), RET_HEADS * RET_DV),
        "kv_norm_g": 1.0 + 0.02 * jax.random.normal(ks[10], (D,), f32),
        "kv_ada_w": w(ks[11], (D, 2 * D), D, 0.5),
        "kv_ada_b": 0.02 * jax.random.normal(ks[12], (2 * D,), f32),
        "kv_w": w(ks[13], (D, KV_COLS), D),
        "attn_w_q": w(ks[14], (N_B_LAYERS, D, Q_COLS), D),
        "attn_w_out": w(ks[15], (N_B_LAYERS, Q_HEADS * HEAD_DIM, D), Q_HEADS * HEAD_DIM),
        "final_norm_g": 1.0 + 0.02 * jax.random.normal(ks[16], (D,), f32),
    }


def reference(x, c, positions, ada_w, ada_b, norm_g, ffn_w_in, ffn_w_out, ret_w_in, ret_w_out,
              kv_norm_g, kv_ada_w, kv_ada_b, kv_w, attn_w_q, attn_w_out, final_norm_g):
    c_act = jax.nn.silu(c)
    k_sh, v_sh = None, None
    for layer in range(DEPTH):
        mod = c_act @ ada_w[layer] + ada_b[layer]
        sh_m, sc_m, gt_m, sh_f, sc_f, gt_f = jnp.split(mod, N_MOD, axis=-1)
        h = modulate(rmsnorm(x, norm_g[layer, 0]), sh_m, sc_m)
        if layer < N_A_LAYERS:
            y = retention_mixer(h, positions, ret_w_in[layer], ret_w_out[layer])
        else:
            lb = layer - N_A_LAYERS
            y = dilated_mixer(h, positions, k_sh, v_sh, attn_w_q[lb], attn_w_out[lb])
        x = x + gt_m[:, None, :] * y
        h = modulate(rmsnorm(x, norm_g[layer, 1]), sh_f, sc_f)
        x = x + gt_f[:, None, :] * swiglu(h, ffn_w_in[layer], ffn_w_out[layer])
        if layer == N_A_LAYERS - 1:
            k_sh, v_sh = shared_kv(x, c, positions, kv_norm_g, kv_ada_w, kv_ada_b, kv_w)
    return rmsnorm(x, final_norm_g)
```

```python
import math
from contextlib import ExitStack

import numpy as np
import ml_dtypes

import concourse.bass as bass
import concourse.mybir as mybir
from concourse.bass_utils import run_bass_kernel_spmd

F32 = mybir.dt.float32
BF16 = mybir.dt.bfloat16
I32 = mybir.dt.int32
AF = mybir.ActivationFunctionType
ALU = mybir.AluOpType

NCORES = 8
D = 2048
S = 16384
T = S // NCORES
NB = 512
NBLK = T // NB
KC = D // 128
EPS = 1e-6
RH = 8
FH = 5632
FKC = FH // 128
TWO_PI = 2.0 * math.pi
C1 = 6.28125
C2 = TWO_PI - C1
GAMMA = [1.0 - 2.0 ** (-5.0 - h) for h in range(RH)]
DIL = ((128, 1), (512, 4), (2048, 16))

COL = {}
for _l in range(2):
    for _i, _n in enumerate(["sh_m", "sc_m", "gt_m", "sh_f", "sc_f", "gt_f"]):
        COL["%s%d" % (_n, _l)] = _l * 96 + _i * 16
COL["sh_kv"] = 192
COL["sc_kv"] = 208
COL["g_m0"] = 224
COL["g_f0"] = 240
COL["g_m1"] = 256
COL["g_f1"] = 272
COL["g_kv"] = 288
COL["g_fin"] = 304
COL["A_m0"] = 320
COL["A_f0"] = 336
COL["A_m1"] = 352
COL["A_f1"] = 368
COL["A_kv"] = 384
NMODC = 400


class Dep:
    __slots__ = ("w", "r")

    def __init__(self):
        self.w = None
        self.r = {}


class KB:
    def __init__(self, nc, nds=20):
        self.nc = nc
        self.es = ExitStack()
        self.engs = {"pe": nc.tensor, "act": nc.scalar, "dve": nc.vector, "pool": nc.gpsimd, "sp": nc.sync}
        self.sem = {}
        self.cnt = {}
        self.seen = {e: {} for e in self.engs}
        for e in self.engs:
            self.sem[e] = self.es.enter_context(nc.semaphore("s_" + e))
            self.cnt[e] = 0
        self.dring = []
        for i in range(nds):
            self.dring.append([self.es.enter_context(nc.semaphore("d%d" % i)), 0])
        self.dnext = 0
        self.nbank = 0
        self.reserved = set()
        self.banks = []
        for i in range(8):
            t = self.es.enter_context(nc.psum_tensor("bank%d" % i, [128, 512], F32))
            self.banks.append((t, Dep()))

    def bank(self):
        while (self.nbank % 8) in self.reserved:
            self.nbank += 1
        b = self.banks[self.nbank % 8]
        self.nbank += 1
        return b

    def reserve(self):
        while (self.nbank % 8) in self.reserved:
            self.nbank += 1
        i = self.nbank % 8
        self.reserved.add(i)
        self.nbank += 1
        return self.banks[i], i

    def sb(self, stack, name, shape, dtype):
        self.nsb = getattr(self, "nsb", 0) + 1
        return stack.enter_context(self.nc.sbuf_tensor("sb%d_%s" % (self.nsb, name), shape, dtype))

    def _wait(self, e, key, sem, val):
        if self.seen[e].get(key, 0) >= val:
            return
        self.engs[e].wait_ge(sem, val)
        self.seen[e][key] = val

    def _dep(self, e, tag, raw):
        key, sem, val, src = tag
        if src == e:
            if e == "pe" or not raw:
                return
        self._wait(e, key, sem, val)

    def _deps(self, e, reads, writes):
        for d in reads:
            if d.w is not None:
                self._dep(e, d.w, True)
        for d in writes:
            if d.w is not None:
                self._dep(e, d.w, False)
            for tag in d.r.values():
                self._dep(e, tag, False)

    def op(self, e, fn, reads=(), writes=()):
        self._deps(e, reads, writes)
        ins = fn(self.engs[e])
        self.cnt[e] += 1
        ins.then_inc(self.sem[e], 1)
        tag = ("s_" + e, self.sem[e], self.cnt[e], e)
        for d in reads:
            d.r[e] = tag
        for d in writes:
            d.w = tag
            d.r = {}
        return ins

    def dma(self, q, out, in_, reads=(), writes=()):
        self._deps(q, reads, writes)
        idx = self.dnext
        slot = self.dring[idx]
        self.dnext = (self.dnext + 1) % len(self.dring)
        key = "d%d" % idx
        if slot[1] > 0:
            self._wait(q, key, slot[0], slot[1])
        self.engs[q].dma_start(out=out, in_=in_).then_inc(slot[0], 16)
        slot[1] += 16
        tag = (key, slot[0], slot[1], "dma")
        for d in reads:
            d.r[key] = tag
        for d in writes:
            d.w = tag
            d.r = {}

    def collective(self, th_in, th_out):
        self.barrier()
        if not hasattr(self, "csem"):
            self.csem = self.es.enter_context(self.nc.semaphore("ccsem"))
            self.ccnt = 0
        self.nc.gpsimd.collective_compute("AllGather", ALU.bypass, replica_groups=[list(range(NCORES))],
                                          ins=[th_in.ap().opt()], outs=[th_out.ap().opt()]).then_inc(self.csem)
        self.ccnt += 1
        for e in self.engs:
            self._wait(e, "ccsem", self.csem, self.ccnt)

    def barrier(self):
        for e in self.engs:
            for f in self.engs:
                if f != e and self.cnt[f] > 0:
                    self._wait(e, "s_" + f, self.sem[f], self.cnt[f])
            for i, (sem, c) in enumerate(self.dring):
                if c > 0:
                    self._wait(e, "d%d" % i, sem, c)

    def mm(self, out, lhsT, rhs, start, stop, reads, writes):
        return self.op("pe", lambda g: g.matmul(out, lhsT, rhs, start=start, stop=stop), reads, writes)

    def tr(self, out, in_, ident, reads, writes):
        return self.op("pe", lambda g: g.transpose(out, in_, ident), reads, writes)

    def act(self, out, in_, func, reads, writes, bias=0.0, scale=1.0, accum_out=None):
        if accum_out is None:
            return self.op("act", lambda g: g.activation(out=out, in_=in_, func=func, bias=bias, scale=scale),
                           reads, writes)
        return self.op("act", lambda g: g.activation(out=out, in_=in_, func=func, bias=bias, scale=scale,
                                                     accum_out=accum_out), reads, writes)

    def tt(self, e, out, in0, in1, op, reads, writes):
        return self.op(e, lambda g: g.tensor_tensor(out=out, in0=in0, in1=in1, op=op), reads, writes)

    def ts(self, e, out, in0, s1, op0, reads, writes, s2=None, op1=None):
        if op1 is None:
            return self.op(e, lambda g: g.tensor_scalar(out=out, in0=in0, scalar1=s1, scalar2=None, op0=op0),
                           reads, writes)
        return self.op(e, lambda g: g.tensor_scalar(out=out, in0=in0, scalar1=s1, scalar2=s2, op0=op0, op1=op1),
                       reads, writes)

    def stt(self, e, out, in0, scalar, in1, op0, op1, reads, writes):
        return self.op(e, lambda g: g.scalar_tensor_tensor(out=out, in0=in0, scalar=scalar, in1=in1,
                                                           op0=op0, op1=op1), reads, writes)

    def copy(self, e, out, in_, reads, writes):
        if e == "act":
            return self.op("act", lambda g: g.copy(out=out, in_=in_), reads, writes)
        return self.op(e, lambda g: g.tensor_copy(out=out, in_=in_), reads, writes)


class Ctx:
    pass


def setup_common(k, cx):
    nc = k.nc
    es = k.es
    cx.modT = k.sb(es, "modT", [128, NMODC], F32)
    cx.modT_d = Dep()
    cx.ones_bf = k.sb(es, "ones_bf", [128, 128], BF16)
    cx.ident = k.sb(es, "ident", [128, 128], BF16)
    cx.identf = k.sb(es, "identf", [128, 128], F32)
    cx.one11 = k.sb(es, "one11", [1, 1], F32)
    cx.const_d = Dep()
    k.op("dve", lambda g: g.memset(cx.ones_bf[:], 1.0), (), (cx.const_d,))
    k.op("dve", lambda g: g.memset(cx.one11[:], 1.0), (), (cx.const_d,))
    k.dma("sp", cx.identf[:], cx.dr["ident"][:, :], (), (cx.const_d,))
    k.copy("dve", cx.ident[:], cx.identf[:], (cx.const_d,), (cx.const_d,))
    cx.wring = []
    for i in range(2):
        cx.wring.append((k.sb(es, "wring%d" % i, [128, 8192], BF16), Dep()))
    cx.wnext = 0


def load_w(k, cx, W, n0, ncols, kc):
    buf, dep = cx.wring[cx.wnext % len(cx.wring)]
    cx.wnext += 1
    view = buf[:, 0:kc * ncols].rearrange("p (k n) -> p k n", n=ncols)
    src = W[:, n0:n0 + ncols].rearrange("(k p) n -> p k n", p=128)
    for k0 in range(0, kc, 16):
        k1 = min(kc, k0 + 16)
        k.dma("pool", view[:, k0:k1, :], src[:, k0:k1, :], (), (dep,))
    return view, dep


def phase0(k, cx, gemv, gains):
    nc = k.nc
    with ExitStack() as st:
        crow = k.sb(st, "crow", [1, D], F32)
        crow_d = Dep()
        cT = k.sb(st, "cT", [128, KC], F32)
        cT_d = Dep()
        wf = [(k.sb(st, "wf%d" % i, [128, KC, 512], F32), Dep()) for i in range(2)]
        brow = [(k.sb(st, "brow%d" % i, [1, 512], F32), Dep()) for i in range(2)]
        rowb = [(k.sb(st, "rowb%d" % i, [1, 512], F32), Dep()) for i in range(2)]
        grow = [(k.sb(st, "grow%d" % i, [1, D], F32), Dep()) for i in range(2)]
        (mps, mps_d), mps_i = k.reserve()
        k.dma("sp", crow[:], cx.dr["c"][:, :], (), (crow_d,))
        k.act(crow[:], crow[:], AF.Silu, (crow_d,), (crow_d,))
        cps, cps_d = k.bank()
        for j in range(KC):
            k.mm(cps[:, j:j + 1], crow[0:1, j * 128:(j + 1) * 128], cx.one11[0:1, 0:1], True, True,
                 (crow_d, cx.const_d), (cps_d,))
        k.copy("dve", cT[:], cps[:, 0:KC], (cps_d,), (cT_d,))
        ti = 0
        for (W, B, cbase) in gemv:
            n = W.shape[1]
            for t in range(n // 512):
                wt, wd = wf[ti % 2]
                bt, bd = brow[ti % 2]
                rt, rd = rowb[ti % 2]
                ti += 1
                k.dma("sp", wt[:], W[:, t * 512:(t + 1) * 512].rearrange("(k p) n -> p k n", p=128), (), (wd,))
                k.dma("sp", bt[:], B[:, t * 512:(t + 1) * 512], (), (bd,))
                rp, rp_d = k.bank()
                for kc in range(KC):
                    k.mm(rp[0:1, :], cT[:, kc:kc + 1], wt[:, kc, :], kc == 0, kc == KC - 1, (cT_d, wd), (rp_d,))
                k.tt("dve", rt[:], rp[0:1, :], bt[:], ALU.add, (rp_d, bd), (rd,))
                for j in range(4):
                    col = cbase + t * 4 + j
                    k.mm(mps[:, col:col + 1], rt[0:1, j * 128:(j + 1) * 128], cx.one11[0:1, 0:1], True, True,
                         (rd, cx.const_d), (mps_d,))
        for gi, (G, cbase) in enumerate(gains):
            gt, gd = grow[gi % 2]
            k.dma("sp", gt[:], G[:, :], (), (gd,))
            for j in range(KC):
                k.mm(mps[:, cbase + j:cbase + j + 1], gt[0:1, j * 128:(j + 1) * 128], cx.one11[0:1, 0:1],
                     True, True, (gd, cx.const_d), (mps_d,))
        k.copy("dve", cx.modT[:, 0:320], mps[:, 0:320], (mps_d,), (cx.modT_d,))
        for (a, g, s) in (("A_m0", "g_m0", "sc_m0"), ("A_f0", "g_f0", "sc_f0"), ("A_m1", "g_m1", "sc_m1"),
                          ("A_f1", "g_f1", "sc_f1"), ("A_kv", "g_kv", "sc_kv")):
            k.stt("dve", cx.modT[:, COL[a]:COL[a] + 16], cx.modT[:, COL[s]:COL[s] + 16], 1.0,
                  cx.modT[:, COL[g]:COL[g] + 16], ALU.add, ALU.mult, (cx.modT_d,), (cx.modT_d,))
        k.barrier()
        k.reserved.discard(mps_i)


def alloc_norm(k, st, cx):
    cx.XB = k.sb(st, "XB", [128, KC, NB], F32)
    cx.XB_d = [Dep() for _ in range(KC)]
    cx.HT = k.sb(st, "HT", [128, KC, NB], BF16)
    cx.HT_d = Dep()
    cx.SQ = [(k.sb(st, "SQ%d" % i, [128, NB], BF16), Dep()) for i in range(2)]
    cx.RS = k.sb(st, "RS", [128, NB], F32)
    cx.RS_d = Dep()
    cx.TMP = [(k.sb(st, "TMP%d" % i, [128, NB], F32), Dep()) for i in range(2)]


def load_xb(k, cx, src, tok0):
    k.dma("sp", cx.XB[:], src[:, tok0:tok0 + NB].rearrange("(k p) t -> p k t", p=128), (), tuple(cx.XB_d))


def norm_block(k, cx, acol, bcol, out_final=None):
    ps, ps_d = k.bank()
    for kc in range(KC):
        sq, sq_d = cx.SQ[kc % 2]
        k.act(sq[:], cx.XB[:, kc, :], AF.Square, (cx.XB_d[kc],), (sq_d,))
        k.mm(ps[:], cx.ones_bf[:], sq[:], kc == 0, kc == KC - 1, (sq_d, cx.const_d), (ps_d,))
    k.act(cx.RS[:], ps[:], AF.Sqrt, (ps_d,), (cx.RS_d,), bias=EPS, scale=1.0 / D)
    k.op("dve", lambda g: g.reciprocal(out=cx.RS[:], in_=cx.RS[:]), (cx.RS_d,), (cx.RS_d,))
    for kc in range(KC):
        if out_final is not None:
            k.stt("dve", cx.XB[:, kc, :], cx.XB[:, kc, :], cx.modT[:, acol + kc:acol + kc + 1], cx.RS[:],
                  ALU.mult, ALU.mult, (cx.XB_d[kc], cx.RS_d, cx.modT_d), (cx.XB_d[kc],))
            continue
        tm, tm_d = cx.TMP[kc % 2]
        k.stt("dve", tm[:], cx.XB[:, kc, :], cx.modT[:, acol + kc:acol + kc + 1], cx.RS[:],
              ALU.mult, ALU.mult, (cx.XB_d[kc], cx.RS_d, cx.modT_d), (tm_d,))
        k.act(cx.HT[:, kc, :], tm[:], AF.Identity, (tm_d, cx.modT_d), (cx.HT_d,),
              bias=cx.modT[:, bcol + kc:bcol + kc + 1], scale=1.0)


def range_reduce_sin(k, out, ang, ang_d, U, U_d, KF, KF_d, shape_slices, reads_extra, out_d, cosine):
    off = 0.25 if cosine else 0.0
    k.ts("dve", U, ang, 1.0 / TWO_PI, ALU.mult, (ang_d,), (U_d,), s2=off, op1=ALU.add)
    k.copy("dve", KF, U, (U_d,), (KF_d,))
    k.stt("dve", out, KF, -C1, ang, ALU.mult, ALU.add, (KF_d, ang_d), (out_d,))
    k.stt("dve", out, KF, -C2, out, ALU.mult, ALU.add, (KF_d, out_d), (out_d,))
    k.act(out, out, AF.Sin, (out_d,), (out_d,), bias=(math.pi / 2 if cosine else 0.0), scale=1.0)


def sweep_ret(k, cx, mode):
    nc = k.nc
    dr = cx.dr
    full = mode == "full"
    with ExitStack() as st:
        alloc_norm(k, st, cx)
        U = k.sb(st, "U", [128, NB], I32)
        U_d = Dep()
        QT = k.sb(st, "QT", [128, 2, NB], BF16)
        QT_d = Dep()
        KT = k.sb(st, "KT", [128, 2, NB], BF16)
        KT_d = Dep()
        KTOK = k.sb(st, "KTOK", [128, 4 * 256], BF16)
        KTOK_d = Dep()
        V = k.sb(st, "V", [128, 4, 512], BF16)
        V_d = [Dep() for _ in range(4)]
        SG = k.sb(st, "SG", [128, 4, 512], BF16)
        SG_d = [Dep() for _ in range(4)]
        STb = [(k.sb(st, "ST%d" % i, [128, 128], BF16), Dep()) for i in range(2)]
        YN = [(k.sb(st, "YN%d" % i, [128, 512], BF16), Dep()) for i in range(2)]
        YTs = [(k.sb(st, "YTs%d" % i, [128, 4, NB], BF16), Dep()) for i in range(2)]
        JUNK = k.sb(st, "JUNK", [128, 512], BF16)
        JUNK_d = Dep()
        R32 = k.sb(st, "R32", [128, 2 * RH, 512], F32)
        R32_d = [Dep() for _ in range(RH)]
        RB = k.sb(st, "RB", [128, 2 * RH, 512], BF16)
        RB_d = [Dep() for _ in range(RH)]
        ssq = [(k.sb(st, "ssq%d" % i, [128, 1], F32), Dep()) for i in range(2)]
        DEC = k.sb(st, "DEC", [128, RH, 2, 128], F32)
        MASKT = k.sb(st, "MASKT", [128, 128], F32)
        INVF = k.sb(st, "INVF", [128, 1], F32)
        POSI = k.sb(st, "POSI", [128, NB], I32)
        POSI_d = Dep()
        cd = Dep()
        k.dma("sp", DEC[:], dr["dec"][:, :].rearrange("p (h a t) -> p h a t", h=RH, a=2), (), (cd,))
        k.dma("sp", MASKT[:], dr["maskt"][:, :], (), (cd,))
        k.dma("sp", INVF[:], dr["invf"][:, :], (), (cd,))
        slot = lambda i: (cx.XB[:, i, :], cx.XB_d[i])
        (POSF, POSF_d), (ANG, ANG_d), (KF, KF_d), (COS, COS_d), (SIN, SIN_d) = [slot(i) for i in range(5)]
        (CQ, CQ_d), (SQh, SQh_d), (CK, CK_d), (SK, SK_d) = [slot(i) for i in range(5, 9)]
        T1, T2, T3, T4 = [slot(i) for i in range(9, 13)]

        if full:
            COEF = k.sb(st, "COEF", [128, 64], F32)
            k.dma("sp", COEF[:], dr["coef"][:, :], (), (cd,))
            k.op("dve", lambda g: g.memset(R32[:], 0.0), (), tuple(R32_d))
            for j in range(NCORES):
                k.dma("sp", cx.XB[:], dr["r_all"][j].rearrange("p (k n) -> p k n", n=512), (), tuple(cx.XB_d))
                for h in range(RH):
                    k.stt("dve", R32[:, 2 * h:2 * h + 2, :], cx.XB[:, 2 * h:2 * h + 2, :],
                          COEF[:, j * 8 + h:j * 8 + h + 1], R32[:, 2 * h:2 * h + 2, :], ALU.mult, ALU.add,
                          (cx.XB_d[2 * h], cx.XB_d[2 * h + 1], R32_d[h], cd), (R32_d[h],))
            for h in range(RH):
                k.copy("act", RB[:, 2 * h:2 * h + 2, :], R32[:, 2 * h:2 * h + 2, :], (R32_d[h],), (RB_d[h],))
        else:
            k.op("dve", lambda g: g.memset(R32[:], 0.0), (), tuple(R32_d))

        for b in range(NBLK):
            tok0 = b * NB
            load_xb(k, cx, dr["xT"], tok0)
            k.dma("sp", POSI[:], dr["pos_rep"][:, tok0:tok0 + NB], (), (POSI_d,))
            norm_block(k, cx, COL["A_m0"], COL["sh_m0"])
            k.copy("dve", POSF, POSI[:], (POSI_d,), (POSF_d,))
            k.ts("dve", ANG, POSF, INVF[:, 0:1], ALU.mult, (POSF_d, cd), (ANG_d,))
            range_reduce_sin(k, SIN, ANG, ANG_d, U[:], U_d, KF, KF_d, None, (), SIN_d, False)
            range_reduce_sin(k, COS, ANG, ANG_d, U[:], U_d, KF, KF_d, None, (), COS_d, True)
            for h in range(RH):
                gC = GAMMA[h] ** 128
                dk = DEC[:, h, 1, :].unsqueeze(1).to_broadcast([128, 4, 128])
                k.tt("dve", CK.rearrange("p (c t) -> p c t", t=128), COS.rearrange("p (c t) -> p c t", t=128), dk,
                     ALU.mult, (COS_d, cd), (CK_d,))
                k.tt("dve", SK.rearrange("p (c t) -> p c t", t=128), SIN.rearrange("p (c t) -> p c t", t=128), dk,
                     ALU.mult, (SIN_d, cd), (SK_d,))
                if full:
                    dq = DEC[:, h, 0, :].unsqueeze(1).to_broadcast([128, 4, 128])
                    k.tt("dve", CQ.rearrange("p (c t) -> p c t", t=128), COS.rearrange("p (c t) -> p c t", t=128),
                         dq, ALU.mult, (COS_d, cd), (CQ_d,))
                    k.tt("dve", SQh.rearrange("p (c t) -> p c t", t=128), SIN.rearrange("p (c t) -> p c t", t=128),
                         dq, ALU.mult, (SIN_d, cd), (SQh_d,))
                W = dr["ret_w_in"]

                def rot_proj(col0, Ctab, Ctab_d, Stab, Stab_d, OUT, OUT_d):
                    wt, wd = load_w(k, cx, W, col0, 256, KC)
                    p0, p0_d = k.bank()
                    p1, p1_d = k.bank()
                    for (pp, pd, c0) in ((p0, p0_d, 0), (p1, p1_d, 128)):
                        for kc in range(KC):
                            k.mm(pp[:], wt[:, kc, c0:c0 + 128], cx.HT[:, kc, :], kc == 0, kc == KC - 1,
                                 (wd, cx.HT_d), (pd,))
                    k.tt("dve", T1[0], p0[:], Ctab, ALU.mult, (p0_d, Ctab_d), (T1[1],))
                    k.tt("dve", T2[0], p1[:], Stab, ALU.mult, (p1_d, Stab_d), (T2[1],))
                    k.tt("dve", OUT[:, 0, :], T1[0], T2[0], ALU.subtract, (T1[1], T2[1]), (OUT_d,))
                    k.tt("dve", T3[0], p1[:], Ctab, ALU.mult, (p1_d, Ctab_d), (T3[1],))
                    k.tt("dve", T4[0], p0[:], Stab, ALU.mult, (p0_d, Stab_d), (T4[1],))
                    k.tt("dve", OUT[:, 1, :], T3[0], T4[0], ALU.add, (T3[1], T4[1]), (OUT_d,))

                if full:
                    rot_proj(h * 256, CQ, CQ_d, SQh, SQh_d, QT, QT_d)
                rot_proj(2048 + h * 256, CK, CK_d, SK, SK_d, KT, KT_d)
                pb, pb_d = k.bank()
                pbv = pb[:].bitcast(BF16)
                for n in range(4):
                    for dc in range(2):
                        o = (n * 2 + dc) * 128
                        k.tr(pbv[:, o:o + 128], KT[:, dc, n * 128:(n + 1) * 128], cx.ident[:],
                             (KT_d, cx.const_d), (pb_d,))
                k.act(KTOK[:], pbv[:, 0:1024], AF.Copy, (pb_d,), (KTOK_d,), scale=gC)
                wv, wv_d = load_w(k, cx, W, 4096 + h * 512, 512, KC)
                for n in range(4):
                    pv, pv_d = k.bank()
                    for kc in range(KC):
                        k.mm(pv[:], cx.HT[:, kc, n * 128:(n + 1) * 128], wv[:, kc, :], kc == 0, kc == KC - 1,
                             (wv_d, cx.HT_d), (pv_d,))
                    k.copy("act", V[:, n, :], pv[:], (pv_d,), (V_d[n],))
                if full:
                    wg, wg_d = load_w(k, cx, W, 8192 + h * 512, 512, KC)
                    for n in range(4):
                        pg, pg_d = k.bank()
                        for kc in range(KC):
                            k.mm(pg[:], cx.HT[:, kc, n * 128:(n + 1) * 128], wg[:, kc, :], kc == 0, kc == KC - 1,
                                 (wg_d, cx.HT_d), (pg_d,))
                        k.act(SG[:, n, :], pg[:], AF.Silu, (pg_d,), (SG_d[n],))
                    yts, yts_d = YTs[h % 2]
                for n in range(4):
                    tk = slice(n * 128, (n + 1) * 128)
                    if full:
                        ps_s, ps_s_d = k.bank()
                        for dc in range(2):
                            k.mm(ps_s[:, 0:128], KT[:, dc, tk], QT[:, dc, tk], dc == 0, dc == 1,
                                 (KT_d, QT_d), (ps_s_d,))
                        stb, stb_d = STb[n % 2]
                        k.tt("dve", stb[:], ps_s[:, 0:128], MASKT[:], ALU.mult, (ps_s_d, cd), (stb_d,))
                        ps_o, ps_o_d = k.bank()
                        k.mm(ps_o[:], stb[:], V[:, n, :], True, False, (stb_d, V_d[n]), (ps_o_d,))
                        for dc in range(2):
                            k.mm(ps_o[:], QT[:, dc, tk], RB[:, 2 * h + dc, :], False, dc == 1,
                                 (QT_d, RB_d[h]), (ps_o_d,))
                        sq1, sq1_d = ssq[n % 2]
                        k.op("dve", lambda g: g.memset(sq1[:], 0.0), (), (sq1_d,))
                        k.act(JUNK[:], ps_o[:], AF.Square, (ps_o_d, sq1_d), (JUNK_d, sq1_d), accum_out=sq1[:])
                        k.act(sq1[:], sq1[:], AF.Sqrt, (sq1_d,), (sq1_d,), bias=EPS, scale=1.0 / 512)
                        k.op("dve", lambda g: g.reciprocal(out=sq1[:], in_=sq1[:]), (sq1_d,), (sq1_d,))
                        yn, yn_d = YN[n % 2]
                        k.stt("dve", yn[:], ps_o[:], sq1[:, 0:1], SG[:, n, :], ALU.mult, ALU.mult,
                              (ps_o_d, sq1_d, SG_d[n]), (yn_d,))
                        pt, pt_d = k.bank()
                        ptv = pt[:].bitcast(BF16)
                        for ec in range(4):
                            k.tr(ptv[:, ec * 128:(ec + 1) * 128], yn[:, ec * 128:(ec + 1) * 128], cx.ident[:],
                                 (yn_d, cx.const_d), (pt_d,))
                        k.copy("act", yts[:, :, tk], ptv[:, 0:512].rearrange("p (e t) -> p e t", t=128),
                               (pt_d,), (yts_d,))
                    for dc in range(2):
                        pS, pS_d = k.bank()
                        o = n * 256 + dc * 128
                        k.mm(pS[:], KTOK[:, o:o + 128], V[:, n, :], True, True, (KTOK_d, V_d[n]), (pS_d,))
                        k.stt("dve", R32[:, 2 * h + dc, :], R32[:, 2 * h + dc, :], gC, pS[:], ALU.mult, ALU.add,
                              (R32_d[h], pS_d), (R32_d[h],))
                    if full:
                        k.copy("act", RB[:, 2 * h:2 * h + 2, :], R32[:, 2 * h:2 * h + 2, :], (R32_d[h],), (RB_d[h],))
                if full:
                    k.dma("sp", dr["yT"][h * 512:(h + 1) * 512, tok0:tok0 + NB].rearrange("(e p) t -> p e t", p=128),
                          yts[:], (yts_d,), ())
        if not full:
            k.dma("sp", dr["s_local"][:, :].rearrange("p (k n) -> p k n", n=512), R32[:], tuple(R32_d), ())
        k.barrier()


def ffn_block(k, cx, BIG, BIG_d, SGT, w_in, w_out, gcol):
    for j4 in range(FKC // 4):
        wg, wg_d = load_w(k, cx, w_in, j4 * 512, 512, KC)
        wu, wu_d = load_w(k, cx, w_in, FH + j4 * 512, 512, KC)
        for jj in range(4):
            j = j4 * 4 + jj
            pg, pg_d = k.bank()
            pu, pu_d = k.bank()
            for kc in range(KC):
                k.mm(pg[:], wg[:, kc, jj * 128:(jj + 1) * 128], cx.HT[:, kc, :], kc == 0, kc == KC - 1,
                     (wg_d, cx.HT_d), (pg_d,))
            for kc in range(KC):
                k.mm(pu[:], wu[:, kc, jj * 128:(jj + 1) * 128], cx.HT[:, kc, :], kc == 0, kc == KC - 1,
                     (wu_d, cx.HT_d), (pu_d,))
            sg, sg_d = SGT[j % 2]
            k.act(sg[:], pg[:], AF.Silu, (pg_d,), (sg_d,))
            k.tt("dve", BIG[:, j, :], sg[:], pu[:], ALU.mult, (sg_d, pu_d), (BIG_d[j],))
    for oc in range(KC):
        wo, wo_d = load_w(k, cx, w_out, oc * 128, 128, FKC)
        po, po_d = k.bank()
        for j in range(FKC):
            k.mm(po[:], wo[:, j, :], BIG[:, j, :], j == 0, j == FKC - 1, (wo_d, BIG_d[j]), (po_d,))
        k.stt("dve", cx.XB[:, oc, :], po[:], cx.modT[:, gcol + oc:gcol + oc + 1], cx.XB[:, oc, :],
              ALU.mult, ALU.add, (po_d, cx.modT_d, cx.XB_d[oc]), (cx.XB_d[oc],))


def attn_tables(k, st, cx, name):
    dr = cx.dr
    PT = k.sb(st, name + "PT", [128, 16], I32)
    PF = k.sb(st, name + "PF", [128, 16], F32)
    IF = k.sb(st, name + "IF", [128, 16], F32)
    AN = k.sb(st, name + "AN", [128, 16, 16], F32)
    UA = k.sb(st, name + "UA", [128, 16, 16], I32)
    KA = k.sb(st, name + "KA", [128, 16, 16], F32)
    cx.COSA = k.sb(st, name + "COSA", [128, 16, 16], F32)
    cx.SINA = k.sb(st, name + "SINA", [128, 16, 16], F32)
    d = Dep()
    an_d, ua_d, ka_d = Dep(), Dep(), Dep()
    cx.rotA_d = Dep()
    k.dma("sp", PT[:], dr["pos_tm"][:, :], (), (d,))
    k.dma("sp", IF[:], dr["invf16"][:, :], (), (d,))
    k.copy("dve", PF[:], PT[:], (d,), (d,))
    for c in range(16):
        k.ts("dve", AN[:, c, :], IF[:], PF[:, c:c + 1], ALU.mult, (d,), (an_d,))
    sdep, cdep = Dep(), Dep()
    range_reduce_sin(k, cx.SINA[:], AN[:], an_d, UA[:], ua_d, KA[:], ka_d, None, (), sdep, False)
    range_reduce_sin(k, cx.COSA[:], AN[:], an_d, UA[:], ua_d, KA[:], ka_d, None, (), cdep, True)
    cx.COSX = k.sb(st, name + "COSX", [128, 16, 4, 16], F32)
    cx.SINX = k.sb(st, name + "SINX", [128, 16, 4, 16], F32)
    cx.XF = (k.sb(st, name + "XF", [128, 512], F32), Dep())
    xd = Dep()
    for h in range(4):
        k.copy("dve", cx.COSX[:, :, h, :], cx.COSA[:], (cdep,), (xd,))
        k.copy("dve", cx.SINX[:, :, h, :], cx.SINA[:], (sdep,), (xd,))
    k.barrier()


def rot_tm(k, cx, OUT, OUT_d, ps, ps_d, chunk, RT, RT_d):
    XF, XF_d = cx.XF
    k.copy("act", XF[:], ps[:], (ps_d,), (XF_d,))
    k.copy("act", OUT[:], ps[:], (ps_d,), (OUT_d,))
    xv = XF[:].rearrange("p (h d) -> p h d", d=128)
    ov = OUT[:].rearrange("p (h d) -> p h d", d=128)
    cb = cx.COSX[:, chunk, :, :]
    sb_ = cx.SINX[:, chunk, :, :]
    x1 = xv[:, :, 0:16]
    x2 = xv[:, :, 16:32]
    k.tt("dve", RT[:, 0], x1, cb, ALU.mult, (XF_d,), (RT_d,))
    k.tt("dve", RT[:, 1], x2, sb_, ALU.mult, (XF_d,), (RT_d,))
    k.tt("dve", RT[:, 2], x2, cb, ALU.mult, (XF_d,), (RT_d,))
    k.tt("dve", RT[:, 3], x1, sb_, ALU.mult, (XF_d,), (RT_d,))
    k.tt("dve", RT[:, 0], RT[:, 0], RT[:, 1], ALU.subtract, (RT_d,), (RT_d,))
    k.tt("dve", RT[:, 2], RT[:, 2], RT[:, 3], ALU.add, (RT_d,), (RT_d,))
    k.copy("dve", ov[:, :, 0:16], RT[:, 0], (RT_d, OUT_d), (OUT_d,))
    k.copy("dve", ov[:, :, 16:32], RT[:, 2], (RT_d, OUT_d), (OUT_d,))


def sweep_o(k, cx):
    dr = cx.dr
    KCUT = 9
    NBL = NBLK
    with ExitStack() as st:
        attn_tables(k, st, cx, "o")
        alloc_norm(k, st, cx)
        BIG = k.sb(st, "BIG", [128, FKC, NB], BF16)
        BIG_d = [Dep() for _ in range(FKC)]
        SGT = [(k.sb(st, "SGT%d" % i, [128, NB], F32), Dep()) for i in range(2)]
        KVO = [(k.sb(st, "KVO%d" % i, [128, 512], BF16), Dep()) for i in range(2)]
        RT = k.sb(st, "RT", [128, 4, 4, 16], F32)
        RT_d = Dep()
        for b in range(NBL):
            tok0 = b * NB
            if KCUT < 1:
                break
            load_xb(k, cx, dr["xT"], tok0)
            for k0 in (0, 16):
                k.dma("sp", BIG[:, k0:k0 + 16, :],
                      dr["yT"][k0 * 128:(k0 + 16) * 128, tok0:tok0 + NB].rearrange("(k p) t -> p k t", p=128),
                      (), tuple(BIG_d[k0:k0 + 16]))
            for oc2 in range(KC // 2):
                wo, wo_d = load_w(k, cx, dr["ret_w_out"], oc2 * 256, 256, 32)
                for o2 in range(2):
                    oc = oc2 * 2 + o2
                    po, po_d = k.bank()
                    for kc in range(32):
                        k.mm(po[:], wo[:, kc, o2 * 128:(o2 + 1) * 128], BIG[:, kc, :], kc == 0, kc == 31,
                             (wo_d, BIG_d[kc]), (po_d,))
                    g = COL["gt_m0"]
                    k.stt("dve", cx.XB[:, oc, :], po[:], cx.modT[:, g + oc:g + oc + 1], cx.XB[:, oc, :],
                          ALU.mult, ALU.add, (po_d, cx.modT_d, cx.XB_d[oc]), (cx.XB_d[oc],))
            if "xmix" in dr:
                k.dma("sp", dr["xmix"][:, tok0:tok0 + NB].rearrange("(k p) t -> p k t", p=128), cx.XB[:],
                      tuple(cx.XB_d), ())
            if KCUT < 2:
                continue
            norm_block(k, cx, COL["A_f0"], COL["sh_f0"])
            ffn_block(k, cx, BIG, BIG_d, SGT, dr["ffn_w_in"], dr["ffn_w_out"], COL["gt_f0"])
            k.dma("sp", dr["x_out"][:, tok0:tok0 + NB].rearrange("(k p) t -> p k t", p=128), cx.XB[:],
                  tuple(cx.XB_d), ())
            if KCUT < 3:
                continue
            norm_block(k, cx, COL["A_kv"], COL["sh_kv"])
            for cg in range(6):
                wt, wd = load_w(k, cx, dr["kv_w"], cg * 512, 512, KC)
                g = cg // 2
                for n in range(4):
                    ps, ps_d = k.bank()
                    for kc in range(KC):
                        k.mm(ps[:], cx.HT[:, kc, n * 128:(n + 1) * 128], wt[:, kc, :], kc == 0, kc == KC - 1,
                             (wd, cx.HT_d), (ps_d,))
                    ko, ko_d = KVO[(cg * 4 + n) % 2]
                    rows = slice(tok0 + n * 128, tok0 + (n + 1) * 128)
                    if cg % 2 == 0:
                        rot_tm(k, cx, ko, ko_d, ps, ps_d, b * 4 + n, RT, RT_d)
                        k.dma("sp", dr["k_out"][rows, g * 512:(g + 1) * 512], ko[:], (ko_d,), ())
                    else:
                        k.copy("act", ko[:], ps[:], (ps_d,), (ko_d,))
                        k.dma("sp", dr["v_out"][rows, g * 512:(g + 1) * 512], ko[:], (ko_d,), ())
        k.barrier()


def sweep_q(k, cx):
    dr = cx.dr
    with ExitStack() as st:
        attn_tables(k, st, cx, "q")
        alloc_norm(k, st, cx)
        QO = [(k.sb(st, "QO%d" % i, [128, 512], BF16), Dep()) for i in range(2)]
        RT = k.sb(st, "RT", [128, 4, 4, 16], F32)
        RT_d = Dep()
        for b in range(NBLK):
            tok0 = b * NB
            load_xb(k, cx, dr["xT"], tok0)
            norm_block(k, cx, COL["A_m1"], COL["sh_m1"])
            for cg in range(12):
                wt, wd = load_w(k, cx, dr["attn_w_q"], cg * 512, 512, KC)
                for n in range(4):
                    ps, ps_d = k.bank()
                    for kc in range(KC):
                        k.mm(ps[:], cx.HT[:, kc, n * 128:(n + 1) * 128], wt[:, kc, :], kc == 0, kc == KC - 1,
                             (wd, cx.HT_d), (ps_d,))
                    qo, qo_d = QO[(cg * 4 + n) % 2]
                    rot_tm(k, cx, qo, qo_d, ps, ps_d, b * 4 + n, RT, RT_d)
                    rows = slice(tok0 + n * 128, tok0 + (n + 1) * 128)
                    k.dma("sp", dr["q_tm"][rows, cg * 512:(cg + 1) * 512], qo[:], (qo_d,), ())
        k.barrier()


def kv_prep(k, cx, gst, gi, dil, nblk, KTg, VAg, kd):
    dr = cx.dr
    KS = [(k.sb(gst, "KS%d" % i, [128, 512], BF16), Dep()) for i in range(2)]
    SL = [(k.sb(gst, "SL%d" % i, [128, 512], BF16), Dep()) for i in range(2)]
    VS = [(k.sb(gst, "VS%d" % i, [128, 512], BF16), Dep()) for i in range(2)]
    OH = k.sb(gst, "OH", [128, 8], F32)
    HV = k.sb(gst, "HV", [128, 1], F32)
    ohd = Dep()
    k.dma("sp", OH[:], dr["onehot"][:, :], (), (ohd,))
    k.dma("sp", HV[:], dr["hv"][:, :], (), (ohd,))
    k.op("dve", lambda g: g.memset(VAg[:, :, :, 128:129], 1.0), (), (kd,))
    cols = slice(gi * 512, (gi + 1) * 512)
    kown = dr["k_tm"].rearrange("(n d) c -> d n c", d=dil)
    vown = dr["v_tm"].rearrange("(n d) c -> d n c", d=dil)
    it = 0
    for r in range(dil):
        for kb in range(nblk + 1):
            kbg = r * (nblk + 1) + kb
            ks, ks_d = KS[it % 2]
            vs, vs_d = VS[it % 2]
            it += 1
            if kb >= 1:
                rows = slice((kb - 1) * 128, kb * 128)
                k.dma("sp", ks[:], kown[r, rows, cols], (), (ks_d,))
                k.dma("sp", vs[:], vown[r, rows, cols], (), (vs_d,))
            else:
                rows = slice((nblk - 1) * 128, nblk * 128)
                for (src, acc, acc_d) in (("kg", ks, ks_d), ("vg", vs, vs_d)):
                    for j in range(NCORES - 1):
                        sl, sl_d = SL[j % 2]
                        sv = dr[src][j * T:(j + 1) * T, :].rearrange("(n d) c -> d n c", d=dil)
                        k.dma("sp", sl[:], sv[r, rows, cols], (), (sl_d,))
                        if j == 0:
                            k.ts("dve", acc[:], sl[:], OH[:, 0:1], ALU.mult, (sl_d, ohd), (acc_d,))
                        else:
                            k.stt("dve", acc[:], sl[:], OH[:, j:j + 1], acc[:], ALU.mult, ALU.add,
                                  (sl_d, ohd, acc_d), (acc_d,))
                k.ts("dve", VAg[:, kbg, :, 128:129], VAg[:, kbg, :, 128:129], HV[:, 0:1], ALU.mult,
                     (kd, ohd), (kd,))
            k.copy("dve", VAg[:, kbg, :, 0:128], vs[:].rearrange("p (h d) -> p h d", d=128), (vs_d,), (kd,))
            pb, pb_d = k.bank()
            pbv = pb[:].bitcast(BF16)
            for h in range(4):
                k.tr(pbv[:, h * 128:(h + 1) * 128], ks[:, h * 128:(h + 1) * 128], cx.ident[:],
                     (ks_d, cx.const_d), (pb_d,))
            k.copy("act", KTg[:, :, kbg * 128:(kbg + 1) * 128],
                   pbv[:, 0:512].rearrange("p (h t) -> p h t", t=128), (pb_d,), (kd,))


def sweep_attn(k, cx):
    dr = cx.dr
    with ExitStack() as st:
        MASK = k.sb(st, "MASK", [128, 2, 128], BF16)
        MASKf = k.sb(st, "MASKf", [128, 2, 128], F32)
        md = Dep()
        k.dma("sp", MASKf[:], dr["amask"][:, :].rearrange("p (a t) -> p a t", a=2), (), (md,))
        k.copy("dve", MASK[:], MASKf[:], (md,), (md,))
        QB = [(k.sb(st, "QB%d" % i, [128, 16 * 128], BF16), Dep()) for i in range(2)]
        QTt = [(k.sb(st, "QTt%d" % i, [128, 16, 128], BF16), Dep()) for i in range(2)]
        PT = [(k.sb(st, "PT%d" % i, [128, 2, 512], BF16), Dep()) for i in range(2)]
        OZ = [(k.sb(st, "OZ%d" % i, [128, 16, 129], F32), Dep()) for i in range(2)]
        it = 0
        for gi, (win, dil) in enumerate(DIL):
            nblk = T // (128 * dil)
            nkb = dil * (nblk + 1)
            gst = ExitStack()
            KTg = k.sb(gst, "KTg%d" % gi, [128, 4, nkb * 128], BF16)
            VAg = k.sb(gst, "VAg%d" % gi, [128, nkb, 4, 129], BF16)
            qv = dr["q_tm"].rearrange("(n d) c -> d n c", d=dil)
            ozv = dr["oz"][gi].rearrange("(n d) c -> d n c", d=dil)
            kd = Dep()
            if "kg" not in dr:
                k.dma("sp", KTg[:], dr["ktg%d" % gi][:, :].rearrange("p (h t) -> p h t", h=4), (), (kd,))
                k.dma("sp", VAg[:], dr["vag%d" % gi][:, :].rearrange("p (b h d) -> p b h d", h=4, d=129), (), (kd,))
            else:
                kv_prep(k, cx, gst, gi, dil, nblk, KTg, VAg, kd)
            for r in range(dil):
                for i in range(nblk):
                    qb, qb_d = QB[it % 2]
                    qt, qt_d = QTt[it % 2]
                    oz, oz_d = OZ[it % 2]
                    it += 1
                    t0 = i * 128 * dil + r
                    qsrc = qv[r, i * 128:(i + 1) * 128, gi * 2048:(gi + 1) * 2048]
                    k.dma("sp", qb[:], qsrc, (), (qb_d,))
                    for half in range(2):
                        pb, pb_d = k.bank()
                        pbv = pb[:].bitcast(BF16)
                        for hh in range(8):
                            hd = half * 8 + hh
                            k.tr(pbv[:, hh * 128:(hh + 1) * 128], qb[:, hd * 128:(hd + 1) * 128], cx.ident[:],
                                 (qb_d, cx.const_d), (pb_d,))
                        k.copy("act", qt[:, half * 8:half * 8 + 8, :],
                               pbv[:, 0:1024].rearrange("p (h t) -> p h t", t=128), (pb_d,), (qt_d,))
                    kb_prev = r * (nblk + 1) + i
                    for kvh in range(4):
                        pt, pt_d = PT[kvh % 2]
                        for a in range(2):
                            ps, ps_d = k.bank()
                            kb = kb_prev + a
                            k.mm(ps[:], KTg[:, kvh, kb * 128:(kb + 1) * 128], qt[:, kvh * 4:kvh * 4 + 4, :],
                                 True, True, (kd, qt_d), (ps_d,))
                            k.act(pt[:, a, :], ps[:], AF.Exp, (ps_d,), (pt_d,), scale=1.0 / math.sqrt(128.0))
                        k.tt("dve", pt[:].rearrange("p a (h t) -> p a h t", t=128),
                             pt[:].rearrange("p a (h t) -> p a h t", t=128),
                             MASK[:].unsqueeze(2).to_broadcast([128, 2, 4, 128]), ALU.mult, (pt_d, md), (pt_d,))
                        for h2 in range(2):
                            po, po_d = k.bank()
                            for hh in range(2):
                                hq = h2 * 2 + hh
                                for a in range(2):
                                    k.mm(po[:, hh * 129:(hh + 1) * 129], pt[:, a, hq * 128:(hq + 1) * 128],
                                         VAg[:, kb_prev + a, kvh, :], a == 0, a == 1, (pt_d, kd), (po_d,))
                            hd0 = kvh * 4 + h2 * 2
                            k.copy("act", oz[:, hd0:hd0 + 2, :], po[:, 0:258].rearrange("p (h d) -> p h d", d=129),
                                   (po_d,), (oz_d,))
                    odst = ozv[r, i * 128:(i + 1) * 128, :]
                    k.dma("sp", odst, oz[:].rearrange("p h d -> p (h d)"), (oz_d,), ())
            k.barrier()
            gst.close()
        k.barrier()


def sweep_l1(k, cx):
    dr = cx.dr
    with ExitStack() as st:
        alloc_norm(k, st, cx)
        BIG = k.sb(st, "BIG", [128, FKC, NB], BF16)
        BIG_d = [Dep() for _ in range(FKC)]
        SGT = [(k.sb(st, "SGT%d" % i, [128, NB], F32), Dep()) for i in range(2)]
        OZA = k.sb(st, "OZA", [128, 16, 129], F32)
        OZB = k.sb(st, "OZB", [128, 16, 129], F32)
        oza_d, ozb_d = Dep(), Dep()
        RZ = k.sb(st, "RZ", [128, 16, 1], F32)
        rz_d = Dep()
        OTM = k.sb(st, "OTM", [128, 16, 128], BF16)
        otm_d = Dep()
        for b in range(NBLK):
            tok0 = b * NB
            load_xb(k, cx, dr["xT"], tok0)
            for n in range(4):
                rows = slice(tok0 + n * 128, tok0 + (n + 1) * 128)
                k.dma("sp", OZA[:].rearrange("p h d -> p (h d)"), dr["oz"][0, rows, :], (), (oza_d,))
                for gi in (1, 2):
                    k.dma("sp", OZB[:].rearrange("p h d -> p (h d)"), dr["oz"][gi, rows, :], (), (ozb_d,))
                    k.tt("dve", OZA[:], OZA[:], OZB[:], ALU.add, (oza_d, ozb_d), (oza_d,))
                k.op("dve", lambda g: g.reciprocal(out=RZ[:], in_=OZA[:, :, 128:129]), (oza_d,), (rz_d,))
                k.tt("dve", OTM[:], OZA[:, :, 0:128], RZ[:].to_broadcast([128, 16, 128]), ALU.mult,
                     (oza_d, rz_d), (otm_d,))
                for half in range(2):
                    pb, pb_d = k.bank()
                    pbv = pb[:].bitcast(BF16)
                    for hh in range(8):
                        k.tr(pbv[:, hh * 128:(hh + 1) * 128], OTM[:, half * 8 + hh, :], cx.ident[:],
                             (otm_d, cx.const_d), (pb_d,))
                    k.copy("act", BIG[:, half * 8:half * 8 + 8, n * 128:(n + 1) * 128],
                           pbv[:, 0:1024].rearrange("p (h t) -> p h t", t=128), (pb_d,),
                           tuple(BIG_d[half * 8:half * 8 + 8]))
            for oc4 in range(KC // 4):
                wo, wo_d = load_w(k, cx, dr["attn_w_out"], oc4 * 512, 512, KC)
                for o4 in range(4):
                    oc = oc4 * 4 + o4
                    po, po_d = k.bank()
                    for kc in range(KC):
                        k.mm(po[:], wo[:, kc, o4 * 128:(o4 + 1) * 128], BIG[:, kc, :], kc == 0, kc == KC - 1,
                             (wo_d, BIG_d[kc]), (po_d,))
                    g = COL["gt_m1"]
                    k.stt("dve", cx.XB[:, oc, :], po[:], cx.modT[:, g + oc:g + oc + 1], cx.XB[:, oc, :],
                          ALU.mult, ALU.add, (po_d, cx.modT_d, cx.XB_d[oc]), (cx.XB_d[oc],))
            if "xmix" in dr:
                k.dma("sp", dr["xmix"][:, tok0:tok0 + NB].rearrange("(k p) t -> p k t", p=128), cx.XB[:],
                      tuple(cx.XB_d), ())
            norm_block(k, cx, COL["A_f1"], COL["sh_f1"])
            ffn_block(k, cx, BIG, BIG_d, SGT, dr["ffn_w_in"], dr["ffn_w_out"], COL["gt_f1"])
            norm_block(k, cx, COL["g_fin"], None, out_final=True)
            k.dma("sp", dr["x_out"][:, tok0:tok0 + NB].rearrange("(k p) t -> p k t", p=128), cx.XB[:],
                  tuple(cx.XB_d), ())
        k.barrier()


def _dt(nc, cx, name, shape, dtype, kind):
    cx.dr[name] = nc.dram_tensor(name, list(shape), dtype, kind=kind).ap()
    return cx.dr[name]


def build(stage, debug_mix=False, parts=None):
    nc = bass.Bass("TRN2", target_bir_lowering=False)
    cx = Ctx()
    cx.dr = {}
    IN = "ExternalInput"
    OUT = "ExternalOutput"
    _dt(nc, cx, "ident", [128, 128], F32, IN)
    _dt(nc, cx, "c", [1, D], F32, IN)
    _dt(nc, cx, "xT", [D, T], F32, IN)
    if stage in (1, 2):
        _dt(nc, cx, "pos_rep", [128, T], I32, IN)
        _dt(nc, cx, "dec", [128, RH * 2 * 128], F32, IN)
        _dt(nc, cx, "maskt", [128, 128], F32, IN)
        _dt(nc, cx, "invf", [128, 1], F32, IN)
        _dt(nc, cx, "ret_w_in", [D, 12288], F32, IN)
    gemv = []
    gains = []
    if stage == 1:
        _dt(nc, cx, "aw", [D, 1024 * 4], F32, IN)
        _dt(nc, cx, "ab", [1, 4096], F32, IN)
        _dt(nc, cx, "g0", [1, D], F32, IN)
        _dt(nc, cx, "s_local", [128, 8192], F32, OUT)
        gemv = [(cx.dr["aw"], cx.dr["ab"], 0)]
        gains = [(cx.dr["g0"], COL["g_m0"])]
    if stage == 2:
        _dt(nc, cx, "aw", [D, 12288], F32, IN)
        _dt(nc, cx, "ab", [1, 12288], F32, IN)
        _dt(nc, cx, "kaw", [D, 4096], F32, IN)
        _dt(nc, cx, "kab", [1, 4096], F32, IN)
        for n_ in ("g0", "g1", "gkv"):
            _dt(nc, cx, n_, [1, D], F32, IN)
        _dt(nc, cx, "pos_tm", [128, 16], I32, IN)
        _dt(nc, cx, "invf16", [128, 16], F32, IN)
        _dt(nc, cx, "coef", [128, 64], F32, IN)
        _dt(nc, cx, "r_all", [NCORES, 128, 8192], F32, IN)
        _dt(nc, cx, "ret_w_out", [4096, D], F32, IN)
        _dt(nc, cx, "ffn_w_in", [D, 2 * FH], F32, IN)
        _dt(nc, cx, "ffn_w_out", [FH, D], F32, IN)
        _dt(nc, cx, "kv_w", [D, 3072], F32, IN)
        _dt(nc, cx, "yT", [4096, T], BF16, OUT)
        _dt(nc, cx, "x_out", [D, T], F32, OUT)
        _dt(nc, cx, "k_out", [T, 1536], BF16, OUT)
        _dt(nc, cx, "v_out", [T, 1536], BF16, OUT)
        if debug_mix:
            _dt(nc, cx, "xmix", [D, T], F32, OUT)
        gemv = [(cx.dr["aw"], cx.dr["ab"], 0), (cx.dr["kaw"], cx.dr["kab"], COL["sh_kv"])]
        gains = [(cx.dr["g0"], COL["g_m0"]), (cx.dr["g1"], COL["g_f0"]), (cx.dr["gkv"], COL["g_kv"])]
    if stage == 3:
        _dt(nc, cx, "aw", [D, 12288], F32, IN)
        _dt(nc, cx, "ab", [1, 12288], F32, IN)
        for n_ in ("g0", "g1", "gfin"):
            _dt(nc, cx, n_, [1, D], F32, IN)
        _dt(nc, cx, "pos_tm", [128, 16], I32, IN)
        _dt(nc, cx, "invf16", [128, 16], F32, IN)
        _dt(nc, cx, "amask", [128, 256], F32, IN)
        _dt(nc, cx, "attn_w_q", [D, 6144], F32, IN)
        _dt(nc, cx, "attn_w_out", [D, D], F32, IN)
        _dt(nc, cx, "ffn_w_in", [D, 2 * FH], F32, IN)
        _dt(nc, cx, "ffn_w_out", [FH, D], F32, IN)
        for gi, (win, dil) in enumerate(DIL):
            nkb = dil * (T // (128 * dil) + 1)
            _dt(nc, cx, "ktg%d" % gi, [128, 4 * nkb * 128], BF16, IN)
            _dt(nc, cx, "vag%d" % gi, [128, nkb * 4 * 129], BF16, IN)
        _dt(nc, cx, "q_tm", [T, 6144], BF16, OUT)
        _dt(nc, cx, "oz", [3, T, 16 * 129], F32, OUT)
        _dt(nc, cx, "x_out", [D, T], F32, OUT)
        if debug_mix:
            _dt(nc, cx, "xmix", [D, T], F32, OUT)
        gemv = [(cx.dr["aw"], cx.dr["ab"], 96)]
        gains = [(cx.dr["g0"], COL["g_m1"]), (cx.dr["g1"], COL["g_f1"]), (cx.dr["gfin"], COL["g_fin"])]
    k = KB(nc)
    setup_common(k, cx)
    k.op("dve", lambda g: g.memset(cx.modT[:], 0.0), (), (cx.modT_d,))
    phase0(k, cx, gemv, gains)
    if stage == 1:
        sweep_ret(k, cx, "state")
    elif stage == 2:
        if parts is None or "ret" in parts:
            sweep_ret(k, cx, "full")
        if parts is None or "o" in parts:
            sweep_o(k, cx)
    else:
        sweep_q(k, cx)
        sweep_attn(k, cx)
        sweep_l1(k, cx)
    k.barrier()
    return nc


def build_fused():
    nc = bass.Bass("TRN2", target_bir_lowering=False)
    cx = Ctx()
    cx.dr = {}
    th = {}
    IN = "ExternalInput"

    def decl(name, shape, dtype, kind):
        th[name] = nc.dram_tensor(name, list(shape), dtype, kind=kind)
        return th[name].ap()

    d = cx.dr
    d["ident"] = decl("ident", [128, 128], F32, IN)
    d["c"] = decl("c", [1, D], F32, IN)
    xin = decl("xT", [D, T], F32, IN)
    d["pos_rep"] = decl("pos_rep", [128, T], I32, IN)
    d["pos_tm"] = decl("pos_tm", [128, 16], I32, IN)
    d["dec"] = decl("dec", [128, RH * 2 * 128], F32, IN)
    d["maskt"] = decl("maskt", [128, 128], F32, IN)
    d["invf"] = decl("invf", [128, 1], F32, IN)
    d["invf16"] = decl("invf16", [128, 16], F32, IN)
    d["amask"] = decl("amask", [128, 256], F32, IN)
    d["coef"] = decl("coef", [128, 64], F32, IN)
    d["onehot"] = decl("onehot", [128, 8], F32, IN)
    d["hv"] = decl("hv", [128, 1], F32, IN)
    aw0 = decl("aw0", [D, 12288], F32, IN)
    ab0 = decl("ab0", [1, 12288], F32, IN)
    aw1 = decl("aw1", [D, 12288], F32, IN)
    ab1 = decl("ab1", [1, 12288], F32, IN)
    kaw = decl("kaw", [D, 4096], F32, IN)
    kab = decl("kab", [1, 4096], F32, IN)
    gs = {n_: decl(n_, [1, D], F32, IN) for n_ in ("g00", "g01", "g10", "g11", "gkv", "gfin")}
    d["ret_w_in"] = decl("ret_w_in", [D, 12288], F32, IN)
    d["ret_w_out"] = decl("ret_w_out", [4096, D], F32, IN)
    fi0 = decl("ffn_w_in0", [D, 2 * FH], F32, IN)
    fo0 = decl("ffn_w_out0", [FH, D], F32, IN)
    fi1 = decl("ffn_w_in1", [D, 2 * FH], F32, IN)
    fo1 = decl("ffn_w_out1", [FH, D], F32, IN)
    d["kv_w"] = decl("kv_w", [D, 3072], F32, IN)
    d["attn_w_q"] = decl("attn_w_q", [D, 6144], F32, IN)
    d["attn_w_out"] = decl("attn_w_out", [D, D], F32, IN)
    INT = "Internal"
    d["s_local"] = decl("s_loc", [128, 8192], F32, INT)
    rall = decl("r_all_i", [NCORES * 128, 8192], F32, INT)
    d["r_all"] = rall.rearrange("(r p) n -> r p n", p=128)
    d["yT"] = decl("yT", [4096, T], BF16, INT)
    x1 = decl("x1", [D, T], F32, INT)
    d["k_tm"] = decl("k_tm", [T, 1536], BF16, INT)
    d["v_tm"] = decl("v_tm", [T, 1536], BF16, INT)
    kg = decl("kg", [NCORES * T, 1536], BF16, INT)
    vg = decl("vg", [NCORES * T, 1536], BF16, INT)
    d["q_tm"] = decl("q_tm", [T, 6144], BF16, INT)
    d["oz"] = decl("oz", [3, T, 16 * 129], F32, INT)
    xout = decl("x_out", [D, T], F32, "ExternalOutput")

    k = KB(nc)
    setup_common(k, cx)
    k.op("dve", lambda g: g.memset(cx.modT[:], 0.0), (), (cx.modT_d,))
    gemv = [(aw0, ab0, 0), (aw1, ab1, 96), (kaw, kab, COL["sh_kv"])]
    gains = [(gs["g00"], COL["g_m0"]), (gs["g01"], COL["g_f0"]), (gs["g10"], COL["g_m1"]),
             (gs["g11"], COL["g_f1"]), (gs["gkv"], COL["g_kv"]), (gs["gfin"], COL["g_fin"])]
    phase0(k, cx, gemv, gains)
    d["xT"] = xin
    sweep_ret(k, cx, "state")
    k.collective(th["s_loc"], th["r_all_i"])
    sweep_ret(k, cx, "full")
    d["ffn_w_in"], d["ffn_w_out"] = fi0, fo0
    d["x_out"] = x1
    d["k_out"], d["v_out"] = d["k_tm"], d["v_tm"]
    sweep_o(k, cx)
    k.collective(th["k_tm"], th["kg"])
    k.collective(th["v_tm"], th["vg"])
    d["kg"], d["vg"] = kg, vg
    d["xT"] = x1
    d["ffn_w_in"], d["ffn_w_out"] = fi1, fo1
    d["x_out"] = xout
    sweep_q(k, cx)
    sweep_attn(k, cx)
    sweep_l1(k, cx)
    k.barrier()
    return nc


def kernel(x, c, positions, ada_w, ada_b, norm_g, ffn_w_in, ffn_w_out, ret_w_in, ret_w_out,
                 kv_norm_g, kv_ada_w, kv_ada_b, kv_w, attn_w_q, attn_w_out, final_norm_g):
    f32 = np.float32
    A = lambda a: np.ascontiguousarray(np.asarray(a))
    x = A(x); c = A(c); positions = A(positions)
    K = _consts()
    pos = positions[0].astype(np.int32)
    ada_w = np.asarray(ada_w); ada_b = np.asarray(ada_b); norm_g = np.asarray(norm_g)
    row = lambda v: A(np.asarray(v).reshape(1, -1))
    common = dict(ident=K["ident"], c=c, dec=K["dec"], maskt=K["maskt"], invf=K["invf"], invf16=K["invf16"],
                  amask=K["amask"], aw0=A(ada_w[0]), ab0=row(ada_b[0]), aw1=A(ada_w[1]), ab1=row(ada_b[1]),
                  kaw=A(kv_ada_w), kab=row(kv_ada_b), g00=row(norm_g[0, 0]), g01=row(norm_g[0, 1]),
                  g10=row(norm_g[1, 0]), g11=row(norm_g[1, 1]), gkv=row(kv_norm_g), gfin=row(final_norm_g),
                  ret_w_in=A(ret_w_in[0]), ret_w_out=A(ret_w_out[0]), ffn_w_in0=A(ffn_w_in[0]),
                  ffn_w_out0=A(ffn_w_out[0]), ffn_w_in1=A(ffn_w_in[1]), ffn_w_out1=A(ffn_w_out[1]),
                  kv_w=A(kv_w), attn_w_q=A(attn_w_q[0]), attn_w_out=A(attn_w_out[0]))
    in_maps = []
    for i in range(NCORES):
        oh = np.zeros((128, 8), f32)
        if i > 0:
            oh[:, i - 1] = 1.0
        hv = np.full((128, 1), 1.0 if i > 0 else 0.0, f32)
        in_maps.append(dict(common, xT=A(x[0, i * T:(i + 1) * T, :].T),
                            pos_rep=A(np.broadcast_to(pos[None, i * T:(i + 1) * T], (128, T))),
                            pos_tm=A(pos[i * T:(i + 1) * T].reshape(16, 128).T),
                            coef=K["coef"][i], onehot=oh, hv=hv))
    nc = build_fused()
    r = _run(nc, in_maps)
    return np.concatenate([r[i]["x_out"].T for i in range(NCORES)], axis=0)[None].astype(f32)


def _consts():
    f32 = np.float32
    idx = np.arange(128, dtype=np.float64)
    dec = np.zeros((128, RH, 2, 128), f32)
    for h in range(RH):
        g = GAMMA[h]
        dec[:, h, 0, :] = (g ** (idx + 1.0)).astype(f32)[None, :]
        dec[:, h, 1, :] = ((g ** (-(idx + 1.0))) * (256.0 ** -0.5)).astype(f32)[None, :]
    maskt = (np.arange(128)[None, :] >= np.arange(128)[:, None]).astype(f32)
    invf = (1.0 / (np.float32(10000.0) ** np.linspace(0.0, 1.0, 128, dtype=f32))).astype(f32).reshape(128, 1)
    invf16 = (np.float32(500000.0) ** (-np.arange(0, 32, 2, dtype=f32) / np.float32(32.0))).astype(f32)
    invf16 = np.broadcast_to(invf16[None, :], (128, 16)).copy()
    kk = np.arange(128)[:, None]
    qq = np.arange(128)[None, :]
    amask = np.stack([(kk >= qq), (kk <= qq)], axis=1).astype(f32).reshape(128, 256)
    coef = np.zeros((NCORES, 128, 64), f32)
    for c in range(NCORES):
        for j in range(c):
            for h in range(RH):
                coef[c, :, j * 8 + h] = GAMMA[h] ** (float(T) * (c - 1 - j))
    return dict(dec=dec.reshape(128, -1), maskt=maskt, invf=invf, invf16=invf16, amask=amask, coef=coef,
                ident=np.eye(128, dtype=f32))


def _run(nc, in_maps):
    res = run_bass_kernel_spmd(nc, in_maps, core_ids=list(range(NCORES)))
    return list(res.results)


def kernel_unfused(x, c, positions, ada_w, ada_b, norm_g, ffn_w_in, ffn_w_out, ret_w_in, ret_w_out,
           kv_norm_g, kv_ada_w, kv_ada_b, kv_w, attn_w_q, attn_w_out, final_norm_g, _debug=None):
    f32 = np.float32
    A = lambda a: np.ascontiguousarray(np.asarray(a))
    x = A(x); c = A(c); positions = A(positions)
    K = _consts()
    xT = [A(x[0, i * T:(i + 1) * T, :].T) for i in range(NCORES)]
    pos = positions[0].astype(np.int32)
    pos_rep = [A(np.broadcast_to(pos[None, i * T:(i + 1) * T], (128, T))) for i in range(NCORES)]
    pos_tm = [A(pos[i * T:(i + 1) * T].reshape(16, 128).T) for i in range(NCORES)]
    ada_w = np.asarray(ada_w); ada_b = np.asarray(ada_b); norm_g = np.asarray(norm_g)
    row = lambda v: A(np.asarray(v).reshape(1, -1))
    dbg = {} if _debug is not None else None

    PARTS = None
    nc1 = build(1)
    common1 = dict(ident=K["ident"], c=c, dec=K["dec"], maskt=K["maskt"], invf=K["invf"],
                   ret_w_in=A(ret_w_in[0]), aw=A(ada_w[0][:, 0:4096]), ab=row(ada_b[0][0:4096]),
                   g0=row(norm_g[0, 0]))
    r1 = _run(nc1, [dict(common1, xT=xT[i], pos_rep=pos_rep[i]) for i in range(NCORES)])
    r_all = A(np.stack([r1[i]["s_local"] for i in range(NCORES)], axis=0))
    if dbg is not None:
        dbg["r_all"] = r_all

    nc2 = build(2, debug_mix=_debug is not None, parts=PARTS)
    common2 = dict(ident=K["ident"], c=c, dec=K["dec"], maskt=K["maskt"], invf=K["invf"], invf16=K["invf16"],
                   ret_w_in=A(ret_w_in[0]), ret_w_out=A(ret_w_out[0]), aw=A(ada_w[0]), ab=row(ada_b[0]),
                   kaw=A(kv_ada_w), kab=row(kv_ada_b), g0=row(norm_g[0, 0]), g1=row(norm_g[0, 1]),
                   gkv=row(kv_norm_g), ffn_w_in=A(ffn_w_in[0]), ffn_w_out=A(ffn_w_out[0]), kv_w=A(kv_w),
                   r_all=r_all)
    r2 = _run(nc2, [dict(common2, xT=xT[i], pos_rep=pos_rep[i], pos_tm=pos_tm[i], coef=K["coef"][i])
                    for i in range(NCORES)])
    if dbg is not None:
        dbg["xmix0"] = np.concatenate([r2[i]["xmix"].T for i in range(NCORES)], axis=0)
        dbg["x_l0"] = np.concatenate([r2[i]["x_out"].T for i in range(NCORES)], axis=0)
        dbg["k"] = np.concatenate([r2[i]["k_out"] for i in range(NCORES)], axis=0)
        dbg["v"] = np.concatenate([r2[i]["v_out"] for i in range(NCORES)], axis=0)
        if _debug == 2:
            return dbg
    bf = ml_dtypes.bfloat16
    Kf = np.concatenate([np.zeros((T, 1536), bf)] + [np.asarray(r2[i]["k_out"]) for i in range(NCORES)], axis=0)
    Vf = np.concatenate([np.zeros((T, 1536), bf)] + [np.asarray(r2[i]["v_out"]) for i in range(NCORES)], axis=0)
    valid = np.concatenate([np.zeros((T,), bf), np.ones((S,), bf)])
    kt_in = [[None] * 3 for _ in range(NCORES)]
    va_in = [[None] * 3 for _ in range(NCORES)]
    for ci in range(NCORES):
        base = T + ci * T
        for gi, (win, dil) in enumerate(DIL):
            nblk = T // (128 * dil)
            r_ = np.arange(dil)[:, None, None]
            kb_ = np.arange(nblk + 1)[None, :, None]
            n_ = np.arange(128)[None, None, :]
            tok = base - 128 * dil + (kb_ * 128 + n_) * dil + r_
            Kg = Kf[tok][..., gi * 512:(gi + 1) * 512]
            Kg = Kg.reshape(dil * (nblk + 1) * 128, 4, 128)
            kt_in[ci][gi] = A(np.transpose(Kg, (2, 1, 0)).reshape(128, -1))
            Vg = Vf[tok][..., gi * 512:(gi + 1) * 512].reshape(dil * (nblk + 1), 128, 4, 128)
            vl = valid[tok].reshape(dil * (nblk + 1), 128, 1, 1)
            Va = np.concatenate([Vg, np.broadcast_to(vl, Vg.shape[:3] + (1,))], axis=-1)
            va_in[ci][gi] = A(np.transpose(Va, (1, 0, 2, 3)).reshape(128, -1))
    nc3 = build(3, debug_mix=_debug is not None)
    common3 = dict(ident=K["ident"], c=c, invf16=K["invf16"], amask=K["amask"], aw=A(ada_w[1]), ab=row(ada_b[1]),
                   g0=row(norm_g[1, 0]), g1=row(norm_g[1, 1]), gfin=row(final_norm_g),
                   attn_w_q=A(attn_w_q[0]), attn_w_out=A(attn_w_out[0]), ffn_w_in=A(ffn_w_in[1]),
                   ffn_w_out=A(ffn_w_out[1]))
    in3 = []
    for i in range(NCORES):
        m = dict(common3, xT=A(r2[i]["x_out"]), pos_tm=pos_tm[i])
        for gi in range(3):
            m["ktg%d" % gi] = kt_in[i][gi]
            m["vag%d" % gi] = va_in[i][gi]
        in3.append(m)
    r3 = _run(nc3, in3)
    out = np.concatenate([r3[i]["x_out"].T for i in range(NCORES)], axis=0)[None].astype(f32)
    if dbg is not None:
        dbg["xmix1"] = np.concatenate([r3[i]["xmix"].T for i in range(NCORES)], axis=0)
        dbg["out"] = out
        return dbg
    return out
```

```python
import math
from contextlib import ExitStack

import numpy as np
import ml_dtypes

import concourse.bass as bass
import concourse.mybir as mybir
from concourse.bass_utils import run_bass_kernel_spmd

F32 = mybir.dt.float32
BF16 = mybir.dt.bfloat16
I32 = mybir.dt.int32
AF = mybir.ActivationFunctionType
ALU = mybir.AluOpType

NCORES = 8
D = 2048
S = 16384
T = S // NCORES
NB = 512
NBLK = T // NB
KC = D // 128
EPS = 1e-6
RH = 8
FH = 5632
FKC = FH // 128
TWO_PI = 2.0 * math.pi
C1 = 6.28125
C2 = TWO_PI - C1
GAMMA = [1.0 - 2.0 ** (-5.0 - h) for h in range(RH)]
DIL = ((128, 1), (512, 4), (2048, 16))

COL = {}
for _l in range(2):
    for _i, _n in enumerate(["sh_m", "sc_m", "gt_m", "sh_f", "sc_f", "gt_f"]):
        COL["%s%d" % (_n, _l)] = _l * 96 + _i * 16
COL["sh_kv"] = 192
COL["sc_kv"] = 208
COL["g_m0"] = 224
COL["g_f0"] = 240
COL["g_m1"] = 256
COL["g_f1"] = 272
COL["g_kv"] = 288
COL["g_fin"] = 304
COL["A_m0"] = 320
COL["A_f0"] = 336
COL["A_m1"] = 352
COL["A_f1"] = 368
COL["A_kv"] = 384
NMODC = 400


class Dep:
    __slots__ = ("w", "r")

    def __init__(self):
        self.w = None
        self.r = {}


class KB:
    def __init__(self, nc, nds=20):
        self.nc = nc
        self.es = ExitStack()
        self.engs = {"pe": nc.tensor, "act": nc.scalar, "dve": nc.vector, "pool": nc.gpsimd, "sp": nc.sync}
        self.sem = {}
        self.cnt = {}
        self.seen = {e: {} for e in self.engs}
        for e in self.engs:
            self.sem[e] = self.es.enter_context(nc.semaphore("s_" + e))
            self.cnt[e] = 0
        self.dring = []
        for i in range(nds):
            self.dring.append([self.es.enter_context(nc.semaphore("d%d" % i)), 0])
        self.dnext = 0
        self.nbank = 0
        self.reserved = set()
        self.banks = []
        for i in range(8):
            t = self.es.enter_context(nc.psum_tensor("bank%d" % i, [128, 512], F32))
            self.banks.append((t, Dep()))

    def bank(self):
        while (self.nbank % 8) in self.reserved:
            self.nbank += 1
        b = self.banks[self.nbank % 8]
        self.nbank += 1
        return b

    def reserve(self):
        while (self.nbank % 8) in self.reserved:
            self.nbank += 1
        i = self.nbank % 8
        self.reserved.add(i)
        self.nbank += 1
        return self.banks[i], i

    def sb(self, stack, name, shape, dtype):
        self.nsb = getattr(self, "nsb", 0) + 1
        return stack.enter_context(self.nc.sbuf_tensor("sb%d_%s" % (self.nsb, name), shape, dtype))

    def _wait(self, e, key, sem, val):
        if self.seen[e].get(key, 0) >= val:
            return
        self.engs[e].wait_ge(sem, val)
        self.seen[e][key] = val

    def _dep(self, e, tag, raw):
        key, sem, val, src = tag
        if src == e:
            if e == "pe" or not raw:
                return
        self._wait(e, key, sem, val)

    def _deps(self, e, reads, writes):
        for d in reads:
            if d.w is not None:
                self._dep(e, d.w, True)
        for d in writes:
            if d.w is not None:
                self._dep(e, d.w, False)
            for tag in d.r.values():
                self._dep(e, tag, False)

    def op(self, e, fn, reads=(), writes=()):
        self._deps(e, reads, writes)
        ins = fn(self.engs[e])
        self.cnt[e] += 1
        ins.then_inc(self.sem[e], 1)
        tag = ("s_" + e, self.sem[e], self.cnt[e], e)
        for d in reads:
            d.r[e] = tag
        for d in writes:
            d.w = tag
            d.r = {}
        return ins

    def dma(self, q, out, in_, reads=(), writes=()):
        self._deps(q, reads, writes)
        idx = self.dnext
        slot = self.dring[idx]
        self.dnext = (self.dnext + 1) % len(self.dring)
        key = "d%d" % idx
        if slot[1] > 0:
            self._wait(q, key, slot[0], slot[1])
        self.engs[q].dma_start(out=out, in_=in_).then_inc(slot[0], 16)
        slot[1] += 16
        tag = (key, slot[0], slot[1], "dma")
        for d in reads:
            d.r[key] = tag
        for d in writes:
            d.w = tag
            d.r = {}

    def collective_wait(self):
        for e in self.engs:
            self._wait(e, "ccsem", self.csem, self.ccnt)

    def collective(self, th_in, th_out, wait=True):
        self.barrier()
        if not hasattr(self, "csem"):
            self.csem = self.es.enter_context(self.nc.semaphore("ccsem"))
            self.ccnt = 0
        self.nc.gpsimd.collective_compute("AllGather", ALU.bypass, replica_groups=[list(range(NCORES))],
                                          ins=[th_in.ap().opt()], outs=[th_out.ap().opt()]).then_inc(self.csem)
        self.ccnt += 1
        if wait:
            self.collective_wait()

    def barrier(self):
        for e in self.engs:
            for f in self.engs:
                if f != e and self.cnt[f] > 0:
                    self._wait(e, "s_" + f, self.sem[f], self.cnt[f])
            for i, (sem, c) in enumerate(self.dring):
                if c > 0:
                    self._wait(e, "d%d" % i, sem, c)

    def mm(self, out, lhsT, rhs, start, stop, reads, writes):
        return self.op("pe", lambda g: g.matmul(out, lhsT, rhs, start=start, stop=stop), reads, writes)

    def tr(self, out, in_, ident, reads, writes):
        return self.op("pe", lambda g: g.transpose(out, in_, ident), reads, writes)

    def act(self, out, in_, func, reads, writes, bias=0.0, scale=1.0, accum_out=None):
        if accum_out is None:
            return self.op("act", lambda g: g.activation(out=out, in_=in_, func=func, bias=bias, scale=scale),
                           reads, writes)
        return self.op("act", lambda g: g.activation(out=out, in_=in_, func=func, bias=bias, scale=scale,
                                                     accum_out=accum_out), reads, writes)

    def tt(self, e, out, in0, in1, op, reads, writes):
        return self.op(e, lambda g: g.tensor_tensor(out=out, in0=in0, in1=in1, op=op), reads, writes)

    def ts(self, e, out, in0, s1, op0, reads, writes, s2=None, op1=None):
        if op1 is None:
            return self.op(e, lambda g: g.tensor_scalar(out=out, in0=in0, scalar1=s1, scalar2=None, op0=op0),
                           reads, writes)
        return self.op(e, lambda g: g.tensor_scalar(out=out, in0=in0, scalar1=s1, scalar2=s2, op0=op0, op1=op1),
                       reads, writes)

    def stt(self, e, out, in0, scalar, in1, op0, op1, reads, writes):
        return self.op(e, lambda g: g.scalar_tensor_tensor(out=out, in0=in0, scalar=scalar, in1=in1,
                                                           op0=op0, op1=op1), reads, writes)

    def copy(self, e, out, in_, reads, writes):
        if e == "act":
            return self.op("act", lambda g: g.copy(out=out, in_=in_), reads, writes)
        return self.op(e, lambda g: g.tensor_copy(out=out, in_=in_), reads, writes)


class Ctx:
    pass


def setup_common(k, cx):
    nc = k.nc
    es = k.es
    cx.modT = k.sb(es, "modT", [128, NMODC], F32)
    cx.modT_d = Dep()
    cx.ones_bf = k.sb(es, "ones_bf", [128, 128], BF16)
    cx.ident = k.sb(es, "ident", [128, 128], BF16)
    cx.identf = k.sb(es, "identf", [128, 128], F32)
    cx.one11 = k.sb(es, "one11", [1, 1], F32)
    cx.const_d = Dep()
    k.op("dve", lambda g: g.memset(cx.ones_bf[:], 1.0), (), (cx.const_d,))
    k.op("dve", lambda g: g.memset(cx.one11[:], 1.0), (), (cx.const_d,))
    k.dma("sp", cx.identf[:], cx.dr["ident"][:, :], (), (cx.const_d,))
    k.copy("dve", cx.ident[:], cx.identf[:], (cx.const_d,), (cx.const_d,))
    cx.wring = []
    for i in range(3):
        cx.wring.append((k.sb(es, "wring%d" % i, [128, 8192], BF16), Dep()))
    cx.wnext = 0


def load_w(k, cx, W, n0, ncols, kc):
    buf, dep = cx.wring[cx.wnext % len(cx.wring)]
    cx.wnext += 1
    view = buf[:, 0:kc * ncols].rearrange("p (k n) -> p k n", n=ncols)
    src = W[:, n0:n0 + ncols].rearrange("(k p) n -> p k n", p=128)
    for k0 in range(0, kc, 16):
        k1 = min(kc, k0 + 16)
        k.dma("pool", view[:, k0:k1, :], src[:, k0:k1, :], (), (dep,))
    return view, dep


def phase0(k, cx, gemv, gains):
    nc = k.nc
    with ExitStack() as st:
        crow = k.sb(st, "crow", [1, D], F32)
        crow_d = Dep()
        cT = k.sb(st, "cT", [128, KC], F32)
        cT_d = Dep()
        wf = [(k.sb(st, "wf%d" % i, [128, KC, 512], F32), Dep()) for i in range(2)]
        brow = [(k.sb(st, "brow%d" % i, [1, 512], F32), Dep()) for i in range(2)]
        rowb = [(k.sb(st, "rowb%d" % i, [1, 512], F32), Dep()) for i in range(2)]
        grow = [(k.sb(st, "grow%d" % i, [1, D], F32), Dep()) for i in range(2)]
        (mps, mps_d), mps_i = k.reserve()
        k.dma("sp", crow[:], cx.dr["c"][:, :], (), (crow_d,))
        k.act(crow[:], crow[:], AF.Silu, (crow_d,), (crow_d,))
        cps, cps_d = k.bank()
        for j in range(KC):
            k.mm(cps[:, j:j + 1], crow[0:1, j * 128:(j + 1) * 128], cx.one11[0:1, 0:1], True, True,
                 (crow_d, cx.const_d), (cps_d,))
        k.copy("dve", cT[:], cps[:, 0:KC], (cps_d,), (cT_d,))
        ti = 0
        for (W, B, cbase) in gemv:
            n = W.shape[1]
            for t in range(n // 512):
                wt, wd = wf[ti % 2]
                bt, bd = brow[ti % 2]
                rt, rd = rowb[ti % 2]
                ti += 1
                k.dma("sp", wt[:], W[:, t * 512:(t + 1) * 512].rearrange("(k p) n -> p k n", p=128), (), (wd,))
                k.dma("sp", bt[:], B[:, t * 512:(t + 1) * 512], (), (bd,))
                rp, rp_d = k.bank()
                for kc in range(KC):
                    k.mm(rp[0:1, :], cT[:, kc:kc + 1], wt[:, kc, :], kc == 0, kc == KC - 1, (cT_d, wd), (rp_d,))
                k.tt("dve", rt[:], rp[0:1, :], bt[:], ALU.add, (rp_d, bd), (rd,))
                for j in range(4):
                    col = cbase + t * 4 + j
                    k.mm(mps[:, col:col + 1], rt[0:1, j * 128:(j + 1) * 128], cx.one11[0:1, 0:1], True, True,
                         (rd, cx.const_d), (mps_d,))
        for gi, (G, cbase) in enumerate(gains):
            gt, gd = grow[gi % 2]
            k.dma("sp", gt[:], G[:, :], (), (gd,))
            for j in range(KC):
                k.mm(mps[:, cbase + j:cbase + j + 1], gt[0:1, j * 128:(j + 1) * 128], cx.one11[0:1, 0:1],
                     True, True, (gd, cx.const_d), (mps_d,))
        k.copy("dve", cx.modT[:, 0:320], mps[:, 0:320], (mps_d,), (cx.modT_d,))
        for (a, g, s) in (("A_m0", "g_m0", "sc_m0"), ("A_f0", "g_f0", "sc_f0"), ("A_m1", "g_m1", "sc_m1"),
                          ("A_f1", "g_f1", "sc_f1"), ("A_kv", "g_kv", "sc_kv")):
            k.stt("dve", cx.modT[:, COL[a]:COL[a] + 16], cx.modT[:, COL[s]:COL[s] + 16], 1.0,
                  cx.modT[:, COL[g]:COL[g] + 16], ALU.add, ALU.mult, (cx.modT_d,), (cx.modT_d,))
        k.barrier()
        k.reserved.discard(mps_i)


def alloc_norm(k, st, cx):
    cx.XB = k.sb(st, "XB", [128, KC, NB], F32)
    cx.XB_d = [Dep() for _ in range(KC)]
    cx.HT = k.sb(st, "HT", [128, KC, NB], BF16)
    cx.HT_d = Dep()
    cx.SQ = [(k.sb(st, "SQ%d" % i, [128, NB], BF16), Dep()) for i in range(2)]
    cx.RS = k.sb(st, "RS", [128, NB], F32)
    cx.RS_d = Dep()
    cx.TMP = [(k.sb(st, "TMP%d" % i, [128, NB], F32), Dep()) for i in range(2)]


def load_xb(k, cx, src, tok0):
    k.dma("sp", cx.XB[:], src[:, tok0:tok0 + NB].rearrange("(k p) t -> p k t", p=128), (), tuple(cx.XB_d))


def norm_block(k, cx, acol, bcol, out_final=None):
    ps, ps_d = k.bank()
    for kc in range(KC):
        sq, sq_d = cx.SQ[kc % 2]
        k.act(sq[:], cx.XB[:, kc, :], AF.Square, (cx.XB_d[kc],), (sq_d,))
        k.mm(ps[:], cx.ones_bf[:], sq[:], kc == 0, kc == KC - 1, (sq_d, cx.const_d), (ps_d,))
    k.act(cx.RS[:], ps[:], AF.Sqrt, (ps_d,), (cx.RS_d,), bias=EPS, scale=1.0 / D)
    k.op("dve", lambda g: g.reciprocal(out=cx.RS[:], in_=cx.RS[:]), (cx.RS_d,), (cx.RS_d,))
    for kc in range(KC):
        if out_final is not None:
            k.stt("dve", cx.XB[:, kc, :], cx.XB[:, kc, :], cx.modT[:, acol + kc:acol + kc + 1], cx.RS[:],
                  ALU.mult, ALU.mult, (cx.XB_d[kc], cx.RS_d, cx.modT_d), (cx.XB_d[kc],))
            continue
        tm, tm_d = cx.TMP[kc % 2]
        k.stt("dve", tm[:], cx.XB[:, kc, :], cx.modT[:, acol + kc:acol + kc + 1], cx.RS[:],
              ALU.mult, ALU.mult, (cx.XB_d[kc], cx.RS_d, cx.modT_d), (tm_d,))
        k.act(cx.HT[:, kc, :], tm[:], AF.Identity, (tm_d, cx.modT_d), (cx.HT_d,),
              bias=cx.modT[:, bcol + kc:bcol + kc + 1], scale=1.0)


def range_reduce_sin(k, out, ang, ang_d, U, U_d, KF, KF_d, shape_slices, reads_extra, out_d, cosine):
    off = 0.25 if cosine else 0.0
    k.ts("dve", U, ang, 1.0 / TWO_PI, ALU.mult, (ang_d,), (U_d,), s2=off, op1=ALU.add)
    k.copy("dve", KF, U, (U_d,), (KF_d,))
    k.stt("dve", out, KF, -C1, ang, ALU.mult, ALU.add, (KF_d, ang_d), (out_d,))
    k.stt("dve", out, KF, -C2, out, ALU.mult, ALU.add, (KF_d, out_d), (out_d,))
    k.act(out, out, AF.Sin, (out_d,), (out_d,), bias=(math.pi / 2 if cosine else 0.0), scale=1.0)


def sweep_ret(k, cx, mode):
    nc = k.nc
    dr = cx.dr
    full = mode == "full"
    with ExitStack() as st:
        alloc_norm(k, st, cx)
        U = k.sb(st, "U", [128, NB], I32)
        U_d = Dep()
        QT = k.sb(st, "QT", [128, 2, NB], BF16)
        QT_d = Dep()
        KT = k.sb(st, "KT", [128, 2, NB], BF16)
        KT_d = Dep()
        KTOK = k.sb(st, "KTOK", [128, 4 * 256], BF16)
        KTOK_d = Dep()
        V = k.sb(st, "V", [128, 4, 512], BF16)
        V_d = [Dep() for _ in range(4)]
        SG = k.sb(st, "SG", [128, 4, 512], BF16)
        SG_d = [Dep() for _ in range(4)]
        STb = [(k.sb(st, "ST%d" % i, [128, 128], BF16), Dep()) for i in range(2)]
        YN = [(k.sb(st, "YN%d" % i, [128, 512], BF16), Dep()) for i in range(2)]
        YTs = [(k.sb(st, "YTs%d" % i, [128, 4, NB], BF16), Dep()) for i in range(2)]
        JUNK = k.sb(st, "JUNK", [128, 512], BF16)
        JUNK_d = Dep()
        R32 = k.sb(st, "R32", [128, 2 * RH, 512], F32)
        R32_d = [Dep() for _ in range(RH)]
        RB = k.sb(st, "RB", [128, 2 * RH, 512], BF16)
        RB_d = [Dep() for _ in range(RH)]
        ssq = [(k.sb(st, "ssq%d" % i, [128, 1], F32), Dep()) for i in range(2)]
        DEC = k.sb(st, "DEC", [128, RH, 2, 128], F32)
        MASKT = k.sb(st, "MASKT", [128, 128], F32)
        INVF = k.sb(st, "INVF", [128, 1], F32)
        POSI = k.sb(st, "POSI", [128, NB], I32)
        POSI_d = Dep()
        cd = Dep()
        k.dma("sp", DEC[:], dr["dec"][:, :].rearrange("p (h a t) -> p h a t", h=RH, a=2), (), (cd,))
        k.dma("sp", MASKT[:], dr["maskt"][:, :], (), (cd,))
        k.dma("sp", INVF[:], dr["invf"][:, :], (), (cd,))
        slot = lambda i: (cx.XB[:, i, :], cx.XB_d[i])
        (POSF, POSF_d), (ANG, ANG_d), (KF, KF_d), (COS, COS_d), (SIN, SIN_d) = [slot(i) for i in range(5)]
        (CQ, CQ_d), (SQh, SQh_d), (CK, CK_d), (SK, SK_d) = [slot(i) for i in range(5, 9)]
        T1, T2, T3, T4 = [slot(i) for i in range(9, 13)]

        if full:
            COEF = k.sb(st, "COEF", [128, 64], F32)
            k.dma("sp", COEF[:], dr["coef"][:, :], (), (cd,))
            k.op("dve", lambda g: g.memset(R32[:], 0.0), (), tuple(R32_d))
            for j in range(NCORES):
                k.dma("sp", cx.XB[:], dr["r_all"][j].rearrange("p (k n) -> p k n", n=512), (), tuple(cx.XB_d))
                for h in range(RH):
                    k.stt("dve", R32[:, 2 * h:2 * h + 2, :], cx.XB[:, 2 * h:2 * h + 2, :],
                          COEF[:, j * 8 + h:j * 8 + h + 1], R32[:, 2 * h:2 * h + 2, :], ALU.mult, ALU.add,
                          (cx.XB_d[2 * h], cx.XB_d[2 * h + 1], R32_d[h], cd), (R32_d[h],))
            for h in range(RH):
                k.copy("act", RB[:, 2 * h:2 * h + 2, :], R32[:, 2 * h:2 * h + 2, :], (R32_d[h],), (RB_d[h],))
        else:
            k.op("dve", lambda g: g.memset(R32[:], 0.0), (), tuple(R32_d))

        for b in range(NBLK):
            tok0 = b * NB
            load_xb(k, cx, dr["xT"], tok0)
            k.dma("sp", POSI[:], dr["pos_rep"][:, tok0:tok0 + NB], (), (POSI_d,))
            norm_block(k, cx, COL["A_m0"], COL["sh_m0"])
            k.copy("dve", POSF, POSI[:], (POSI_d,), (POSF_d,))
            k.ts("dve", ANG, POSF, INVF[:, 0:1], ALU.mult, (POSF_d, cd), (ANG_d,))
            range_reduce_sin(k, SIN, ANG, ANG_d, U[:], U_d, KF, KF_d, None, (), SIN_d, False)
            range_reduce_sin(k, COS, ANG, ANG_d, U[:], U_d, KF, KF_d, None, (), COS_d, True)
            for h in range(RH):
                gC = GAMMA[h] ** 128
                dk = DEC[:, h, 1, :].unsqueeze(1).to_broadcast([128, 4, 128])
                k.tt("dve", CK.rearrange("p (c t) -> p c t", t=128), COS.rearrange("p (c t) -> p c t", t=128), dk,
                     ALU.mult, (COS_d, cd), (CK_d,))
                k.tt("dve", SK.rearrange("p (c t) -> p c t", t=128), SIN.rearrange("p (c t) -> p c t", t=128), dk,
                     ALU.mult, (SIN_d, cd), (SK_d,))
                if full:
                    dq = DEC[:, h, 0, :].unsqueeze(1).to_broadcast([128, 4, 128])
                    k.tt("dve", CQ.rearrange("p (c t) -> p c t", t=128), COS.rearrange("p (c t) -> p c t", t=128),
                         dq, ALU.mult, (COS_d, cd), (CQ_d,))
                    k.tt("dve", SQh.rearrange("p (c t) -> p c t", t=128), SIN.rearrange("p (c t) -> p c t", t=128),
                         dq, ALU.mult, (SIN_d, cd), (SQh_d,))
                W = dr["ret_w_in"]

                def rot_proj(col0, Ctab, Ctab_d, Stab, Stab_d, OUT, OUT_d):
                    wt, wd = load_w(k, cx, W, col0, 256, KC)
                    p0, p0_d = k.bank()
                    p1, p1_d = k.bank()
                    for (pp, pd, c0) in ((p0, p0_d, 0), (p1, p1_d, 128)):
                        for kc in range(KC):
                            k.mm(pp[:], wt[:, kc, c0:c0 + 128], cx.HT[:, kc, :], kc == 0, kc == KC - 1,
                                 (wd, cx.HT_d), (pd,))
                    k.tt("dve", T1[0], p0[:], Ctab, ALU.mult, (p0_d, Ctab_d), (T1[1],))
                    k.tt("dve", T2[0], p1[:], Stab, ALU.mult, (p1_d, Stab_d), (T2[1],))
                    k.tt("dve", OUT[:, 0, :], T1[0], T2[0], ALU.subtract, (T1[1], T2[1]), (OUT_d,))
                    k.tt("dve", T3[0], p1[:], Ctab, ALU.mult, (p1_d, Ctab_d), (T3[1],))
                    k.tt("dve", T4[0], p0[:], Stab, ALU.mult, (p0_d, Stab_d), (T4[1],))
                    k.tt("dve", OUT[:, 1, :], T3[0], T4[0], ALU.add, (T3[1], T4[1]), (OUT_d,))

                if full:
                    rot_proj(h * 256, CQ, CQ_d, SQh, SQh_d, QT, QT_d)
                rot_proj(2048 + h * 256, CK, CK_d, SK, SK_d, KT, KT_d)
                wv, wv_d = load_w(k, cx, W, 4096 + h * 512, 512, KC)
                for n in range(4):
                    pv, pv_d = k.bank()
                    for kc in range(KC):
                        k.mm(pv[:], cx.HT[:, kc, n * 128:(n + 1) * 128], wv[:, kc, :], kc == 0, kc == KC - 1,
                             (wv_d, cx.HT_d), (pv_d,))
                    k.copy("act", V[:, n, :], pv[:], (pv_d,), (V_d[n],))
                if full:
                    wg, wg_d = load_w(k, cx, W, 8192 + h * 512, 512, KC)
                    for n in range(4):
                        pg, pg_d = k.bank()
                        for kc in range(KC):
                            k.mm(pg[:], cx.HT[:, kc, n * 128:(n + 1) * 128], wg[:, kc, :], kc == 0, kc == KC - 1,
                                 (wg_d, cx.HT_d), (pg_d,))
                        k.act(SG[:, n, :], pg[:], AF.Silu, (pg_d,), (SG_d[n],))
                pb, pb_d = k.bank()
                pbv = pb[:].bitcast(BF16)
                for n in range(4):
                    for dc in range(2):
                        o = (n * 2 + dc) * 128
                        k.tr(pbv[:, o:o + 128], KT[:, dc, n * 128:(n + 1) * 128], cx.ident[:],
                             (KT_d, cx.const_d), (pb_d,))
                k.act(KTOK[:], pbv[:, 0:1024], AF.Copy, (pb_d,), (KTOK_d,), scale=gC)
                if full:
                    yts, yts_d = YTs[h % 2]
                for n in range(4):
                    tk = slice(n * 128, (n + 1) * 128)
                    if full:
                        ps_s, ps_s_d = k.bank()
                        for dc in range(2):
                            k.mm(ps_s[:, 0:128], KT[:, dc, tk], QT[:, dc, tk], dc == 0, dc == 1,
                                 (KT_d, QT_d), (ps_s_d,))
                        stb, stb_d = STb[n % 2]
                        k.tt("dve", stb[:], ps_s[:, 0:128], MASKT[:], ALU.mult, (ps_s_d, cd), (stb_d,))
                        ps_o, ps_o_d = k.bank()
                        k.mm(ps_o[:], stb[:], V[:, n, :], True, False, (stb_d, V_d[n]), (ps_o_d,))
                        for dc in range(2):
                            k.mm(ps_o[:], QT[:, dc, tk], RB[:, 2 * h + dc, :], False, dc == 1,
                                 (QT_d, RB_d[h]), (ps_o_d,))
                        sq1, sq1_d = ssq[n % 2]
                        k.op("dve", lambda g: g.memset(sq1[:], 0.0), (), (sq1_d,))
                        k.act(JUNK[:], ps_o[:], AF.Square, (ps_o_d, sq1_d), (JUNK_d, sq1_d), accum_out=sq1[:])
                        k.act(sq1[:], sq1[:], AF.Sqrt, (sq1_d,), (sq1_d,), bias=EPS, scale=1.0 / 512)
                        k.op("dve", lambda g: g.reciprocal(out=sq1[:], in_=sq1[:]), (sq1_d,), (sq1_d,))
                        yn, yn_d = YN[n % 2]
                        k.stt("dve", yn[:], ps_o[:], sq1[:, 0:1], SG[:, n, :], ALU.mult, ALU.mult,
                              (ps_o_d, sq1_d, SG_d[n]), (yn_d,))
                        pt, pt_d = k.bank()
                        ptv = pt[:].bitcast(BF16)
                        for ec in range(4):
                            k.tr(ptv[:, ec * 128:(ec + 1) * 128], yn[:, ec * 128:(ec + 1) * 128], cx.ident[:],
                                 (yn_d, cx.const_d), (pt_d,))
                        k.copy("act", yts[:, :, tk], ptv[:, 0:512].rearrange("p (e t) -> p e t", t=128),
                               (pt_d,), (yts_d,))
                    for dc in range(2):
                        pS, pS_d = k.bank()
                        o = n * 256 + dc * 128
                        k.mm(pS[:], KTOK[:, o:o + 128], V[:, n, :], True, True, (KTOK_d, V_d[n]), (pS_d,))
                        k.stt("dve", R32[:, 2 * h + dc, :], R32[:, 2 * h + dc, :], gC, pS[:], ALU.mult, ALU.add,
                              (R32_d[h], pS_d), (R32_d[h],))
                    if full:
                        k.copy("act", RB[:, 2 * h:2 * h + 2, :], R32[:, 2 * h:2 * h + 2, :], (R32_d[h],), (RB_d[h],))
                if full:
                    k.dma("sp", dr["yT"][h * 512:(h + 1) * 512, tok0:tok0 + NB].rearrange("(e p) t -> p e t", p=128),
                          yts[:], (yts_d,), ())
        if not full:
            k.dma("sp", dr["s_local"][:, :].rearrange("p (k n) -> p k n", n=512), R32[:], tuple(R32_d), ())
        k.barrier()


def ffn_block(k, cx, BIG, BIG_d, SGT, w_in, w_out, gcol):
    for j4 in range(FKC // 4):
        wg, wg_d = load_w(k, cx, w_in, j4 * 512, 512, KC)
        wu, wu_d = load_w(k, cx, w_in, FH + j4 * 512, 512, KC)
        for jj in range(4):
            j = j4 * 4 + jj
            pg, pg_d = k.bank()
            pu, pu_d = k.bank()
            for kc in range(KC):
                k.mm(pg[:], wg[:, kc, jj * 128:(jj + 1) * 128], cx.HT[:, kc, :], kc == 0, kc == KC - 1,
                     (wg_d, cx.HT_d), (pg_d,))
            for kc in range(KC):
                k.mm(pu[:], wu[:, kc, jj * 128:(jj + 1) * 128], cx.HT[:, kc, :], kc == 0, kc == KC - 1,
                     (wu_d, cx.HT_d), (pu_d,))
            sg, sg_d = SGT[j % 2]
            k.act(sg[:], pg[:], AF.Silu, (pg_d,), (sg_d,))
            k.tt("dve", BIG[:, j, :], sg[:], pu[:], ALU.mult, (sg_d, pu_d), (BIG_d[j],))
    for oc in range(KC):
        wo, wo_d = load_w(k, cx, w_out, oc * 128, 128, FKC)
        po, po_d = k.bank()
        for j in range(FKC):
            k.mm(po[:], wo[:, j, :], BIG[:, j, :], j == 0, j == FKC - 1, (wo_d, BIG_d[j]), (po_d,))
        k.stt("dve", cx.XB[:, oc, :], po[:], cx.modT[:, gcol + oc:gcol + oc + 1], cx.XB[:, oc, :],
              ALU.mult, ALU.add, (po_d, cx.modT_d, cx.XB_d[oc]), (cx.XB_d[oc],))


def attn_tables(k, st, cx, name):
    dr = cx.dr
    PT = k.sb(st, name + "PT", [128, 16], I32)
    PF = k.sb(st, name + "PF", [128, 16], F32)
    IF = k.sb(st, name + "IF", [128, 16], F32)
    AN = k.sb(st, name + "AN", [128, 16, 16], F32)
    UA = k.sb(st, name + "UA", [128, 16, 16], I32)
    KA = k.sb(st, name + "KA", [128, 16, 16], F32)
    cx.COSA = k.sb(st, name + "COSA", [128, 16, 16], F32)
    cx.SINA = k.sb(st, name + "SINA", [128, 16, 16], F32)
    d = Dep()
    an_d, ua_d, ka_d = Dep(), Dep(), Dep()
    cx.rotA_d = Dep()
    k.dma("sp", PT[:], dr["pos_tm"][:, :], (), (d,))
    k.dma("sp", IF[:], dr["invf16"][:, :], (), (d,))
    k.copy("dve", PF[:], PT[:], (d,), (d,))
    for c in range(16):
        k.ts("dve", AN[:, c, :], IF[:], PF[:, c:c + 1], ALU.mult, (d,), (an_d,))
    sdep, cdep = Dep(), Dep()
    range_reduce_sin(k, cx.SINA[:], AN[:], an_d, UA[:], ua_d, KA[:], ka_d, None, (), sdep, False)
    range_reduce_sin(k, cx.COSA[:], AN[:], an_d, UA[:], ua_d, KA[:], ka_d, None, (), cdep, True)
    cx.COSX = k.sb(st, name + "COSX", [128, 16, 4, 16], F32)
    cx.SINX = k.sb(st, name + "SINX", [128, 16, 4, 16], F32)
    cx.XF = (k.sb(st, name + "XF", [128, 512], F32), Dep())
    xd = Dep()
    for h in range(4):
        k.copy("dve", cx.COSX[:, :, h, :], cx.COSA[:], (cdep,), (xd,))
        k.copy("dve", cx.SINX[:, :, h, :], cx.SINA[:], (sdep,), (xd,))
    k.barrier()


def rot_tm(k, cx, OUT, OUT_d, ps, ps_d, chunk, RT, RT_d):
    XF, XF_d = cx.XF
    k.copy("act", XF[:], ps[:], (ps_d,), (XF_d,))
    k.copy("act", OUT[:], ps[:], (ps_d,), (OUT_d,))
    xv = XF[:].rearrange("p (h d) -> p h d", d=128)
    ov = OUT[:].rearrange("p (h d) -> p h d", d=128)
    cb = cx.COSX[:, chunk, :, :]
    sb_ = cx.SINX[:, chunk, :, :]
    x1 = xv[:, :, 0:16]
    x2 = xv[:, :, 16:32]
    k.tt("dve", RT[:, 0], x1, cb, ALU.mult, (XF_d,), (RT_d,))
    k.tt("dve", RT[:, 1], x2, sb_, ALU.mult, (XF_d,), (RT_d,))
    k.tt("dve", RT[:, 2], x2, cb, ALU.mult, (XF_d,), (RT_d,))
    k.tt("dve", RT[:, 3], x1, sb_, ALU.mult, (XF_d,), (RT_d,))
    k.tt("dve", RT[:, 0], RT[:, 0], RT[:, 1], ALU.subtract, (RT_d,), (RT_d,))
    k.tt("dve", RT[:, 2], RT[:, 2], RT[:, 3], ALU.add, (RT_d,), (RT_d,))
    k.copy("dve", ov[:, :, 0:16], RT[:, 0], (RT_d, OUT_d), (OUT_d,))
    k.copy("dve", ov[:, :, 16:32], RT[:, 2], (RT_d, OUT_d), (OUT_d,))


def sweep_o(k, cx):
    dr = cx.dr
    KCUT = 9
    NBL = NBLK
    with ExitStack() as st:
        attn_tables(k, st, cx, "o")
        alloc_norm(k, st, cx)
        BIG = k.sb(st, "BIG", [128, FKC, NB], BF16)
        BIG_d = [Dep() for _ in range(FKC)]
        SGT = [(k.sb(st, "SGT%d" % i, [128, NB], F32), Dep()) for i in range(2)]
        KVO = [(k.sb(st, "KVO%d" % i, [128, 512], BF16), Dep()) for i in range(2)]
        RT = k.sb(st, "RT", [128, 4, 4, 16], F32)
        RT_d = Dep()
        for b in range(NBL):
            tok0 = b * NB
            if KCUT < 1:
                break
            load_xb(k, cx, dr["xT"], tok0)
            for k0 in (0, 16):
                k.dma("sp", BIG[:, k0:k0 + 16, :],
                      dr["yT"][k0 * 128:(k0 + 16) * 128, tok0:tok0 + NB].rearrange("(k p) t -> p k t", p=128),
                      (), tuple(BIG_d[k0:k0 + 16]))
            for oc2 in range(KC // 2):
                wo, wo_d = load_w(k, cx, dr["ret_w_out"], oc2 * 256, 256, 32)
                for o2 in range(2):
                    oc = oc2 * 2 + o2
                    po, po_d = k.bank()
                    for kc in range(32):
                        k.mm(po[:], wo[:, kc, o2 * 128:(o2 + 1) * 128], BIG[:, kc, :], kc == 0, kc == 31,
                             (wo_d, BIG_d[kc]), (po_d,))
                    g = COL["gt_m0"]
                    k.stt("dve", cx.XB[:, oc, :], po[:], cx.modT[:, g + oc:g + oc + 1], cx.XB[:, oc, :],
                          ALU.mult, ALU.add, (po_d, cx.modT_d, cx.XB_d[oc]), (cx.XB_d[oc],))
            if "xmix" in dr:
                k.dma("sp", dr["xmix"][:, tok0:tok0 + NB].rearrange("(k p) t -> p k t", p=128), cx.XB[:],
                      tuple(cx.XB_d), ())
            if KCUT < 2:
                continue
            norm_block(k, cx, COL["A_f0"], COL["sh_f0"])
            ffn_block(k, cx, BIG, BIG_d, SGT, dr["ffn_w_in"], dr["ffn_w_out"], COL["gt_f0"])
            k.dma("sp", dr["x_out"][:, tok0:tok0 + NB].rearrange("(k p) t -> p k t", p=128), cx.XB[:],
                  tuple(cx.XB_d), ())
            if KCUT < 3:
                continue
            norm_block(k, cx, COL["A_kv"], COL["sh_kv"])
            for cg in range(6):
                wt, wd = load_w(k, cx, dr["kv_w"], cg * 512, 512, KC)
                g = cg // 2
                for n in range(4):
                    ps, ps_d = k.bank()
                    for kc in range(KC):
                        k.mm(ps[:], cx.HT[:, kc, n * 128:(n + 1) * 128], wt[:, kc, :], kc == 0, kc == KC - 1,
                             (wd, cx.HT_d), (ps_d,))
                    ko, ko_d = KVO[(cg * 4 + n) % 2]
                    rows = slice(tok0 + n * 128, tok0 + (n + 1) * 128)
                    if cg % 2 == 0:
                        rot_tm(k, cx, ko, ko_d, ps, ps_d, b * 4 + n, RT, RT_d)
                        k.dma("sp", dr["k_out"][rows, g * 512:(g + 1) * 512], ko[:], (ko_d,), ())
                    else:
                        k.copy("act", ko[:], ps[:], (ps_d,), (ko_d,))
                        k.dma("sp", dr["v_out"][rows, g * 512:(g + 1) * 512], ko[:], (ko_d,), ())
        k.barrier()


def sweep_q(k, cx):
    dr = cx.dr
    with ExitStack() as st:
        attn_tables(k, st, cx, "q")
        alloc_norm(k, st, cx)
        QO = [(k.sb(st, "QO%d" % i, [128, 512], BF16), Dep()) for i in range(2)]
        RT = k.sb(st, "RT", [128, 4, 4, 16], F32)
        RT_d = Dep()
        for b in range(NBLK):
            tok0 = b * NB
            load_xb(k, cx, dr["xT"], tok0)
            norm_block(k, cx, COL["A_m1"], COL["sh_m1"])
            for cg in range(12):
                wt, wd = load_w(k, cx, dr["attn_w_q"], cg * 512, 512, KC)
                for n in range(4):
                    ps, ps_d = k.bank()
                    for kc in range(KC):
                        k.mm(ps[:], cx.HT[:, kc, n * 128:(n + 1) * 128], wt[:, kc, :], kc == 0, kc == KC - 1,
                             (wd, cx.HT_d), (ps_d,))
                    qo, qo_d = QO[(cg * 4 + n) % 2]
                    rot_tm(k, cx, qo, qo_d, ps, ps_d, b * 4 + n, RT, RT_d)
                    rows = slice(tok0 + n * 128, tok0 + (n + 1) * 128)
                    k.dma("sp", dr["q_tm"][rows, cg * 512:(cg + 1) * 512], qo[:], (qo_d,), ())
        k.barrier()


def kv_prep(k, cx, gst, gi, dil, nblk, KTg, VAg, kd):
    dr = cx.dr
    KS = [(k.sb(gst, "KS%d" % i, [128, 512], BF16), Dep()) for i in range(2)]
    SL = [(k.sb(gst, "SL%d" % i, [128, NCORES - 1, 512], BF16), Dep()) for i in range(2)]
    VS = [(k.sb(gst, "VS%d" % i, [128, 512], BF16), Dep()) for i in range(2)]
    OH = k.sb(gst, "OH", [128, 8], F32)
    HV = k.sb(gst, "HV", [128, 1], F32)
    ohd = Dep()
    k.dma("sp", OH[:], dr["onehot"][:, :], (), (ohd,))
    k.dma("sp", HV[:], dr["hv"][:, :], (), (ohd,))
    k.op("dve", lambda g: g.memset(VAg[:, :, :, 128:129], 1.0), (), (kd,))
    cols = slice(gi * 512, (gi + 1) * 512)
    kown = dr["k_tm"].rearrange("(n d) c -> d n c", d=dil)
    vown = dr["v_tm"].rearrange("(n d) c -> d n c", d=dil)
    it = 0
    for r in range(dil):
        for kb in range(nblk + 1):
            kbg = r * (nblk + 1) + kb
            ks, ks_d = KS[it % 2]
            vs, vs_d = VS[it % 2]
            it += 1
            if kb >= 1:
                rows = slice((kb - 1) * 128, kb * 128)
                k.dma("sp", ks[:], kown[r, rows, cols], (), (ks_d,))
                k.dma("sp", vs[:], vown[r, rows, cols], (), (vs_d,))
            else:
                rows = slice((nblk - 1) * 128, nblk * 128)
                for si, (src, acc, acc_d) in enumerate((("kg", ks, ks_d), ("vg", vs, vs_d))):
                    sl, sl_d = SL[si]
                    sv = dr[src].rearrange("(j n d) c -> d n j c", j=NCORES, d=dil)
                    k.dma("sp", sl[:], sv[r, rows, 0:NCORES - 1, cols], (), (sl_d,))
                    for j in range(NCORES - 1):
                        if j == 0:
                            k.ts("dve", acc[:], sl[:, 0, :], OH[:, 0:1], ALU.mult, (sl_d, ohd), (acc_d,))
                        else:
                            k.stt("dve", acc[:], sl[:, j, :], OH[:, j:j + 1], acc[:], ALU.mult, ALU.add,
                                  (sl_d, ohd, acc_d), (acc_d,))
                k.ts("dve", VAg[:, kbg, :, 128:129], VAg[:, kbg, :, 128:129], HV[:, 0:1], ALU.mult,
                     (kd, ohd), (kd,))
            k.copy("dve", VAg[:, kbg, :, 0:128], vs[:].rearrange("p (h d) -> p h d", d=128), (vs_d,), (kd,))
            pb, pb_d = k.bank()
            pbv = pb[:].bitcast(BF16)
            for h in range(4):
                k.tr(pbv[:, h * 128:(h + 1) * 128], ks[:, h * 128:(h + 1) * 128], cx.ident[:],
                     (ks_d, cx.const_d), (pb_d,))
            k.copy("act", KTg[:, :, kbg * 128:(kbg + 1) * 128],
                   pbv[:, 0:512].rearrange("p (h t) -> p h t", t=128), (pb_d,), (kd,))


def sweep_attn(k, cx):
    dr = cx.dr
    with ExitStack() as st:
        MASK = k.sb(st, "MASK", [128, 2, 128], BF16)
        MASKf = k.sb(st, "MASKf", [128, 2, 128], F32)
        md = Dep()
        k.dma("sp", MASKf[:], dr["amask"][:, :].rearrange("p (a t) -> p a t", a=2), (), (md,))
        k.copy("dve", MASK[:], MASKf[:], (md,), (md,))
        QB = [(k.sb(st, "QB%d" % i, [128, 16 * 128], BF16), Dep()) for i in range(2)]
        QTt = [(k.sb(st, "QTt%d" % i, [128, 16, 128], BF16), Dep()) for i in range(2)]
        PT = [(k.sb(st, "PT%d" % i, [128, 2, 512], BF16), Dep()) for i in range(2)]
        OZ = [(k.sb(st, "OZ%d" % i, [128, 16, 129], F32), Dep()) for i in range(2)]
        it = 0
        for gi, (win, dil) in enumerate(DIL):
            nblk = T // (128 * dil)
            nkb = dil * (nblk + 1)
            gst = ExitStack()
            KTg = k.sb(gst, "KTg%d" % gi, [128, 4, nkb * 128], BF16)
            VAg = k.sb(gst, "VAg%d" % gi, [128, nkb, 4, 129], BF16)
            qv = dr["q_tm"].rearrange("(n d) c -> d n c", d=dil)
            ozv = dr["oz"][gi].rearrange("(n d) c -> d n c", d=dil)
            kd = Dep()
            if "kg" not in dr:
                k.dma("sp", KTg[:], dr["ktg%d" % gi][:, :].rearrange("p (h t) -> p h t", h=4), (), (kd,))
                k.dma("sp", VAg[:], dr["vag%d" % gi][:, :].rearrange("p (b h d) -> p b h d", h=4, d=129), (), (kd,))
            else:
                kv_prep(k, cx, gst, gi, dil, nblk, KTg, VAg, kd)
            for r in range(dil):
                for i in range(nblk):
                    qb, qb_d = QB[it % 2]
                    qt, qt_d = QTt[it % 2]
                    oz, oz_d = OZ[it % 2]
                    it += 1
                    t0 = i * 128 * dil + r
                    qsrc = qv[r, i * 128:(i + 1) * 128, gi * 2048:(gi + 1) * 2048]
                    k.dma("sp", qb[:], qsrc, (), (qb_d,))
                    for half in range(2):
                        pb, pb_d = k.bank()
                        pbv = pb[:].bitcast(BF16)
                        for hh in range(8):
                            hd = half * 8 + hh
                            k.tr(pbv[:, hh * 128:(hh + 1) * 128], qb[:, hd * 128:(hd + 1) * 128], cx.ident[:],
                                 (qb_d, cx.const_d), (pb_d,))
                        k.copy("act", qt[:, half * 8:half * 8 + 8, :],
                               pbv[:, 0:1024].rearrange("p (h t) -> p h t", t=128), (pb_d,), (qt_d,))
                    kb_prev = r * (nblk + 1) + i
                    for kvh in range(4):
                        pt, pt_d = PT[kvh % 2]
                        for a in range(2):
                            ps, ps_d = k.bank()
                            kb = kb_prev + a
                            k.mm(ps[:], KTg[:, kvh, kb * 128:(kb + 1) * 128], qt[:, kvh * 4:kvh * 4 + 4, :],
                                 True, True, (kd, qt_d), (ps_d,))
                            k.act(pt[:, a, :], ps[:], AF.Exp, (ps_d,), (pt_d,), scale=1.0 / math.sqrt(128.0))
                        k.tt("dve", pt[:].rearrange("p a (h t) -> p a h t", t=128),
                             pt[:].rearrange("p a (h t) -> p a h t", t=128),
                             MASK[:].unsqueeze(2).to_broadcast([128, 2, 4, 128]), ALU.mult, (pt_d, md), (pt_d,))
                        for h2 in range(2):
                            po, po_d = k.bank()
                            for hh in range(2):
                                hq = h2 * 2 + hh
                                for a in range(2):
                                    k.mm(po[:, hh * 129:(hh + 1) * 129], pt[:, a, hq * 128:(hq + 1) * 128],
                                         VAg[:, kb_prev + a, kvh, :], a == 0, a == 1, (pt_d, kd), (po_d,))
                            hd0 = kvh * 4 + h2 * 2
                            k.copy("act", oz[:, hd0:hd0 + 2, :], po[:, 0:258].rearrange("p (h d) -> p h d", d=129),
                                   (po_d,), (oz_d,))
                    odst = ozv[r, i * 128:(i + 1) * 128, :]
                    k.dma("sp", odst, oz[:].rearrange("p h d -> p (h d)"), (oz_d,), ())
            k.barrier()
            gst.close()
        k.barrier()


def sweep_l1(k, cx):
    dr = cx.dr
    with ExitStack() as st:
        alloc_norm(k, st, cx)
        BIG = k.sb(st, "BIG", [128, FKC, NB], BF16)
        BIG_d = [Dep() for _ in range(FKC)]
        SGT = [(k.sb(st, "SGT%d" % i, [128, NB], F32), Dep()) for i in range(2)]
        OZAr = [(k.sb(st, "OZA%d" % i, [128, 16, 129], F32), Dep()) for i in range(2)]
        OZBr = [(k.sb(st, "OZB%d" % i, [128, 16, 129], F32), Dep()) for i in range(2)]
        RZ = k.sb(st, "RZ", [128, 16, 1], F32)
        rz_d = Dep()
        OTM = k.sb(st, "OTM", [128, 16, 128], BF16)
        otm_d = Dep()
        for b in range(NBLK):
            tok0 = b * NB
            load_xb(k, cx, dr["xT"], tok0)
            for n in range(4):
                rows = slice(tok0 + n * 128, tok0 + (n + 1) * 128)
                OZA, oza_d = OZAr[n % 2]
                k.dma("sp", OZA[:].rearrange("p h d -> p (h d)"), dr["oz"][0, rows, :], (), (oza_d,))
                for gi in (1, 2):
                    OZB, ozb_d = OZBr[gi - 1]
                    k.dma("sp", OZB[:].rearrange("p h d -> p (h d)"), dr["oz"][gi, rows, :], (), (ozb_d,))
                for gi in (1, 2):
                    OZB, ozb_d = OZBr[gi - 1]
                    k.tt("dve", OZA[:], OZA[:], OZB[:], ALU.add, (oza_d, ozb_d), (oza_d,))
                k.op("dve", lambda g: g.reciprocal(out=RZ[:], in_=OZA[:, :, 128:129]), (oza_d,), (rz_d,))
                k.tt("dve", OTM[:], OZA[:, :, 0:128], RZ[:].to_broadcast([128, 16, 128]), ALU.mult,
                     (oza_d, rz_d), (otm_d,))
                for half in range(2):
                    pb, pb_d = k.bank()
                    pbv = pb[:].bitcast(BF16)
                    for hh in range(8):
                        k.tr(pbv[:, hh * 128:(hh + 1) * 128], OTM[:, half * 8 + hh, :], cx.ident[:],
                             (otm_d, cx.const_d), (pb_d,))
                    k.copy("act", BIG[:, half * 8:half * 8 + 8, n * 128:(n + 1) * 128],
                           pbv[:, 0:1024].rearrange("p (h t) -> p h t", t=128), (pb_d,),
                           tuple(BIG_d[half * 8:half * 8 + 8]))
            for oc4 in range(KC // 4):
                wo, wo_d = load_w(k, cx, dr["attn_w_out"], oc4 * 512, 512, KC)
                for o4 in range(4):
                    oc = oc4 * 4 + o4
                    po, po_d = k.bank()
                    for kc in range(KC):
                        k.mm(po[:], wo[:, kc, o4 * 128:(o4 + 1) * 128], BIG[:, kc, :], kc == 0, kc == KC - 1,
                             (wo_d, BIG_d[kc]), (po_d,))
                    g = COL["gt_m1"]
                    k.stt("dve", cx.XB[:, oc, :], po[:], cx.modT[:, g + oc:g + oc + 1], cx.XB[:, oc, :],
                          ALU.mult, ALU.add, (po_d, cx.modT_d, cx.XB_d[oc]), (cx.XB_d[oc],))
            if "xmix" in dr:
                k.dma("sp", dr["xmix"][:, tok0:tok0 + NB].rearrange("(k p) t -> p k t", p=128), cx.XB[:],
                      tuple(cx.XB_d), ())
            norm_block(k, cx, COL["A_f1"], COL["sh_f1"])
            ffn_block(k, cx, BIG, BIG_d, SGT, dr["ffn_w_in"], dr["ffn_w_out"], COL["gt_f1"])
            norm_block(k, cx, COL["g_fin"], None, out_final=True)
            k.dma("sp", dr["x_out"][:, tok0:tok0 + NB].rearrange("(k p) t -> p k t", p=128), cx.XB[:],
                  tuple(cx.XB_d), ())
        k.barrier()


def _dt(nc, cx, name, shape, dtype, kind):
    cx.dr[name] = nc.dram_tensor(name, list(shape), dtype, kind=kind).ap()
    return cx.dr[name]


def build(stage, debug_mix=False, parts=None):
    nc = bass.Bass("TRN2", target_bir_lowering=False)
    cx = Ctx()
    cx.dr = {}
    IN = "ExternalInput"
    OUT = "ExternalOutput"
    _dt(nc, cx, "ident", [128, 128], F32, IN)
    _dt(nc, cx, "c", [1, D], F32, IN)
    _dt(nc, cx, "xT", [D, T], F32, IN)
    if stage in (1, 2):
        _dt(nc, cx, "pos_rep", [128, T], I32, IN)
        _dt(nc, cx, "dec", [128, RH * 2 * 128], F32, IN)
        _dt(nc, cx, "maskt", [128, 128], F32, IN)
        _dt(nc, cx, "invf", [128, 1], F32, IN)
        _dt(nc, cx, "ret_w_in", [D, 12288], F32, IN)
    gemv = []
    gains = []
    if stage == 1:
        _dt(nc, cx, "aw", [D, 1024 * 4], F32, IN)
        _dt(nc, cx, "ab", [1, 4096], F32, IN)
        _dt(nc, cx, "g0", [1, D], F32, IN)
        _dt(nc, cx, "s_local", [128, 8192], F32, OUT)
        gemv = [(cx.dr["aw"], cx.dr["ab"], 0)]
        gains = [(cx.dr["g0"], COL["g_m0"])]
    if stage == 2:
        _dt(nc, cx, "aw", [D, 12288], F32, IN)
        _dt(nc, cx, "ab", [1, 12288], F32, IN)
        _dt(nc, cx, "kaw", [D, 4096], F32, IN)
        _dt(nc, cx, "kab", [1, 4096], F32, IN)
        for n_ in ("g0", "g1", "gkv"):
            _dt(nc, cx, n_, [1, D], F32, IN)
        _dt(nc, cx, "pos_tm", [128, 16], I32, IN)
        _dt(nc, cx, "invf16", [128, 16], F32, IN)
        _dt(nc, cx, "coef", [128, 64], F32, IN)
        _dt(nc, cx, "r_all", [NCORES, 128, 8192], F32, IN)
        _dt(nc, cx, "ret_w_out", [4096, D], F32, IN)
        _dt(nc, cx, "ffn_w_in", [D, 2 * FH], F32, IN)
        _dt(nc, cx, "ffn_w_out", [FH, D], F32, IN)
        _dt(nc, cx, "kv_w", [D, 3072], F32, IN)
        _dt(nc, cx, "yT", [4096, T], BF16, OUT)
        _dt(nc, cx, "x_out", [D, T], F32, OUT)
        _dt(nc, cx, "k_out", [T, 1536], BF16, OUT)
        _dt(nc, cx, "v_out", [T, 1536], BF16, OUT)
        if debug_mix:
            _dt(nc, cx, "xmix", [D, T], F32, OUT)
        gemv = [(cx.dr["aw"], cx.dr["ab"], 0), (cx.dr["kaw"], cx.dr["kab"], COL["sh_kv"])]
        gains = [(cx.dr["g0"], COL["g_m0"]), (cx.dr["g1"], COL["g_f0"]), (cx.dr["gkv"], COL["g_kv"])]
    if stage == 3:
        _dt(nc, cx, "aw", [D, 12288], F32, IN)
        _dt(nc, cx, "ab", [1, 12288], F32, IN)
        for n_ in ("g0", "g1", "gfin"):
            _dt(nc, cx, n_, [1, D], F32, IN)
        _dt(nc, cx, "pos_tm", [128, 16], I32, IN)
        _dt(nc, cx, "invf16", [128, 16], F32, IN)
        _dt(nc, cx, "amask", [128, 256], F32, IN)
        _dt(nc, cx, "attn_w_q", [D, 6144], F32, IN)
        _dt(nc, cx, "attn_w_out", [D, D], F32, IN)
        _dt(nc, cx, "ffn_w_in", [D, 2 * FH], F32, IN)
        _dt(nc, cx, "ffn_w_out", [FH, D], F32, IN)
        for gi, (win, dil) in enumerate(DIL):
            nkb = dil * (T // (128 * dil) + 1)
            _dt(nc, cx, "ktg%d" % gi, [128, 4 * nkb * 128], BF16, IN)
            _dt(nc, cx, "vag%d" % gi, [128, nkb * 4 * 129], BF16, IN)
        _dt(nc, cx, "q_tm", [T, 6144], BF16, OUT)
        _dt(nc, cx, "oz", [3, T, 16 * 129], F32, OUT)
        _dt(nc, cx, "x_out", [D, T], F32, OUT)
        if debug_mix:
            _dt(nc, cx, "xmix", [D, T], F32, OUT)
        gemv = [(cx.dr["aw"], cx.dr["ab"], 96)]
        gains = [(cx.dr["g0"], COL["g_m1"]), (cx.dr["g1"], COL["g_f1"]), (cx.dr["gfin"], COL["g_fin"])]
    k = KB(nc)
    setup_common(k, cx)
    k.op("dve", lambda g: g.memset(cx.modT[:], 0.0), (), (cx.modT_d,))
    phase0(k, cx, gemv, gains)
    if stage == 1:
        sweep_ret(k, cx, "state")
    elif stage == 2:
        if parts is None or "ret" in parts:
            sweep_ret(k, cx, "full")
        if parts is None or "o" in parts:
            sweep_o(k, cx)
    else:
        sweep_q(k, cx)
        sweep_attn(k, cx)
        sweep_l1(k, cx)
    k.barrier()
    return nc


def build_fused():
    nc = bass.Bass("TRN2", target_bir_lowering=False)
    cx = Ctx()
    cx.dr = {}
    th = {}
    IN = "ExternalInput"

    def decl(name, shape, dtype, kind):
        th[name] = nc.dram_tensor(name, list(shape), dtype, kind=kind)
        return th[name].ap()

    d = cx.dr
    d["ident"] = decl("ident", [128, 128], F32, IN)
    d["c"] = decl("c", [1, D], F32, IN)
    xin = decl("xT", [D, T], F32, IN)
    d["pos_rep"] = decl("pos_rep", [128, T], I32, IN)
    d["pos_tm"] = decl("pos_tm", [128, 16], I32, IN)
    d["dec"] = decl("dec", [128, RH * 2 * 128], F32, IN)
    d["maskt"] = decl("maskt", [128, 128], F32, IN)
    d["invf"] = decl("invf", [128, 1], F32, IN)
    d["invf16"] = decl("invf16", [128, 16], F32, IN)
    d["amask"] = decl("amask", [128, 256], F32, IN)
    d["coef"] = decl("coef", [128, 64], F32, IN)
    d["onehot"] = decl("onehot", [128, 8], F32, IN)
    d["hv"] = decl("hv", [128, 1], F32, IN)
    aw0 = decl("aw0", [D, 12288], F32, IN)
    ab0 = decl("ab0", [1, 12288], F32, IN)
    aw1 = decl("aw1", [D, 12288], F32, IN)
    ab1 = decl("ab1", [1, 12288], F32, IN)
    kaw = decl("kaw", [D, 4096], F32, IN)
    kab = decl("kab", [1, 4096], F32, IN)
    gs = {n_: decl(n_, [1, D], F32, IN) for n_ in ("g00", "g01", "g10", "g11", "gkv", "gfin")}
    d["ret_w_in"] = decl("ret_w_in", [D, 12288], F32, IN)
    d["ret_w_out"] = decl("ret_w_out", [4096, D], F32, IN)
    fi0 = decl("ffn_w_in0", [D, 2 * FH], F32, IN)
    fo0 = decl("ffn_w_out0", [FH, D], F32, IN)
    fi1 = decl("ffn_w_in1", [D, 2 * FH], F32, IN)
    fo1 = decl("ffn_w_out1", [FH, D], F32, IN)
    d["kv_w"] = decl("kv_w", [D, 3072], F32, IN)
    d["attn_w_q"] = decl("attn_w_q", [D, 6144], F32, IN)
    d["attn_w_out"] = decl("attn_w_out", [D, D], F32, IN)
    INT = "Internal"
    d["s_local"] = decl("s_loc", [128, 8192], F32, INT)
    rall = decl("r_all_i", [NCORES * 128, 8192], F32, INT)
    d["r_all"] = rall.rearrange("(r p) n -> r p n", p=128)
    d["yT"] = decl("yT", [4096, T], BF16, INT)
    x1 = decl("x1", [D, T], F32, INT)
    d["k_tm"] = decl("k_tm", [T, 1536], BF16, INT)
    d["v_tm"] = decl("v_tm", [T, 1536], BF16, INT)
    kg = decl("kg", [NCORES * T, 1536], BF16, INT)
    vg = decl("vg", [NCORES * T, 1536], BF16, INT)
    d["q_tm"] = decl("q_tm", [T, 6144], BF16, INT)
    d["oz"] = decl("oz", [3, T, 16 * 129], F32, INT)
    xout = decl("x_out", [D, T], F32, "ExternalOutput")

    k = KB(nc)
    setup_common(k, cx)
    k.op("dve", lambda g: g.memset(cx.modT[:], 0.0), (), (cx.modT_d,))
    gemv = [(aw0, ab0, 0), (aw1, ab1, 96), (kaw, kab, COL["sh_kv"])]
    gains = [(gs["g00"], COL["g_m0"]), (gs["g01"], COL["g_f0"]), (gs["g10"], COL["g_m1"]),
             (gs["g11"], COL["g_f1"]), (gs["gkv"], COL["g_kv"]), (gs["gfin"], COL["g_fin"])]
    phase0(k, cx, gemv, gains)
    d["xT"] = xin
    sweep_ret(k, cx, "state")
    k.collective(th["s_loc"], th["r_all_i"])
    sweep_ret(k, cx, "full")
    d["ffn_w_in"], d["ffn_w_out"] = fi0, fo0
    d["x_out"] = x1
    d["k_out"], d["v_out"] = d["k_tm"], d["v_tm"]
    sweep_o(k, cx)
    k.collective(th["k_tm"], th["kg"], wait=False)
    k.collective(th["v_tm"], th["vg"], wait=False)
    d["kg"], d["vg"] = kg, vg
    d["xT"] = x1
    d["ffn_w_in"], d["ffn_w_out"] = fi1, fo1
    d["x_out"] = xout
    sweep_q(k, cx)
    k.collective_wait()
    sweep_attn(k, cx)
    sweep_l1(k, cx)
    k.barrier()
    return nc


def kernel(x, c, positions, ada_w, ada_b, norm_g, ffn_w_in, ffn_w_out, ret_w_in, ret_w_out,
                 kv_norm_g, kv_ada_w, kv_ada_b, kv_w, attn_w_q, attn_w_out, final_norm_g):
    f32 = np.float32
    A = lambda a: np.ascontiguousarray(np.asarray(a))
    x = A(x); c = A(c); positions = A(positions)
    K = _consts()
    pos = positions[0].astype(np.int32)
    ada_w = np.asarray(ada_w); ada_b = np.asarray(ada_b); norm_g = np.asarray(norm_g)
    row = lambda v: A(np.asarray(v).reshape(1, -1))
    common = dict(ident=K["ident"], c=c, dec=K["dec"], maskt=K["maskt"], invf=K["invf"], invf16=K["invf16"],
                  amask=K["amask"], aw0=A(ada_w[0]), ab0=row(ada_b[0]), aw1=A(ada_w[1]), ab1=row(ada_b[1]),
                  kaw=A(kv_ada_w), kab=row(kv_ada_b), g00=row(norm_g[0, 0]), g01=row(norm_g[0, 1]),
                  g10=row(norm_g[1, 0]), g11=row(norm_g[1, 1]), gkv=row(kv_norm_g), gfin=row(final_norm_g),
                  ret_w_in=A(ret_w_in[0]), ret_w_out=A(ret_w_out[0]), ffn_w_in0=A(ffn_w_in[0]),
                  ffn_w_out0=A(ffn_w_out[0]), ffn_w_in1=A(ffn_w_in[1]), ffn_w_out1=A(ffn_w_out[1]),
                  kv_w=A(kv_w), attn_w_q=A(attn_w_q[0]), attn_w_out=A(attn_w_out[0]))
    in_maps = []
    for i in range(NCORES):
        oh = np.zeros((128, 8), f32)
        if i > 0:
            oh[:, i - 1] = 1.0
        hv = np.full((128, 1), 1.0 if i > 0 else 0.0, f32)
        in_maps.append(dict(common, xT=A(x[0, i * T:(i + 1) * T, :].T),
                            pos_rep=A(np.broadcast_to(pos[None, i * T:(i + 1) * T], (128, T))),
                            pos_tm=A(pos[i * T:(i + 1) * T].reshape(16, 128).T),
                            coef=K["coef"][i], onehot=oh, hv=hv))
    nc = build_fused()
    r = _run(nc, in_maps)
    return np.concatenate([r[i]["x_out"].T for i in range(NCORES)], axis=0)[None].astype(f32)


def _consts():
    f32 = np.float32
    idx = np.arange(128, dtype=np.float64)
    dec = np.zeros((128, RH, 2, 128), f32)
    for h in range(RH):
        g = GAMMA[h]
        dec[:, h, 0, :] = (g ** (idx + 1.0)).astype(f32)[None, :]
        dec[:, h, 1, :] = ((g ** (-(idx + 1.0))) * (256.0 ** -0.5)).astype(f32)[None, :]
    maskt = (np.arange(128)[None, :] >= np.arange(128)[:, None]).astype(f32)
    invf = (1.0 / (np.float32(10000.0) ** np.linspace(0.0, 1.0, 128, dtype=f32))).astype(f32).reshape(128, 1)
    invf16 = (np.float32(500000.0) ** (-np.arange(0, 32, 2, dtype=f32) / np.float32(32.0))).astype(f32)
    invf16 = np.broadcast_to(invf16[None, :], (128, 16)).copy()
    kk = np.arange(128)[:, None]
    qq = np.arange(128)[None, :]
    amask = np.stack([(kk >= qq), (kk <= qq)], axis=1).astype(f32).reshape(128, 256)
    coef = np.zeros((NCORES, 128, 64), f32)
    for c in range(NCORES):
        for j in range(c):
            for h in range(RH):
                coef[c, :, j * 8 + h] = GAMMA[h] ** (float(T) * (c - 1 - j))
    return dict(dec=dec.reshape(128, -1), maskt=maskt, invf=invf, invf16=invf16, amask=amask, coef=coef,
                ident=np.eye(128, dtype=f32))


def _run(nc, in_maps):
    res = run_bass_kernel_spmd(nc, in_maps, core_ids=list(range(NCORES)))
    return list(res.results)


def kernel_unfused(x, c, positions, ada_w, ada_b, norm_g, ffn_w_in, ffn_w_out, ret_w_in, ret_w_out,
           kv_norm_g, kv_ada_w, kv_ada_b, kv_w, attn_w_q, attn_w_out, final_norm_g, _debug=None):
    f32 = np.float32
    A = lambda a: np.ascontiguousarray(np.asarray(a))
    x = A(x); c = A(c); positions = A(positions)
    K = _consts()
    xT = [A(x[0, i * T:(i + 1) * T, :].T) for i in range(NCORES)]
    pos = positions[0].astype(np.int32)
    pos_rep = [A(np.broadcast_to(pos[None, i * T:(i + 1) * T], (128, T))) for i in range(NCORES)]
    pos_tm = [A(pos[i * T:(i + 1) * T].reshape(16, 128).T) for i in range(NCORES)]
    ada_w = np.asarray(ada_w); ada_b = np.asarray(ada_b); norm_g = np.asarray(norm_g)
    row = lambda v: A(np.asarray(v).reshape(1, -1))
    dbg = {} if _debug is not None else None

    PARTS = None
    nc1 = build(1)
    common1 = dict(ident=K["ident"], c=c, dec=K["dec"], maskt=K["maskt"], invf=K["invf"],
                   ret_w_in=A(ret_w_in[0]), aw=A(ada_w[0][:, 0:4096]), ab=row(ada_b[0][0:4096]),
                   g0=row(norm_g[0, 0]))
    r1 = _run(nc1, [dict(common1, xT=xT[i], pos_rep=pos_rep[i]) for i in range(NCORES)])
    r_all = A(np.stack([r1[i]["s_local"] for i in range(NCORES)], axis=0))
    if dbg is not None:
        dbg["r_all"] = r_all

    nc2 = build(2, debug_mix=_debug is not None, parts=PARTS)
    common2 = dict(ident=K["ident"], c=c, dec=K["dec"], maskt=K["maskt"], invf=K["invf"], invf16=K["invf16"],
                   ret_w_in=A(ret_w_in[0]), ret_w_out=A(ret_w_out[0]), aw=A(ada_w[0]), ab=row(ada_b[0]),
                   kaw=A(kv_ada_w), kab=row(kv_ada_b), g0=row(norm_g[0, 0]), g1=row(norm_g[0, 1]),
                   gkv=row(kv_norm_g), ffn_w_in=A(ffn_w_in[0]), ffn_w_out=A(ffn_w_out[0]), kv_w=A(kv_w),
                   r_all=r_all)
    r2 = _run(nc2, [dict(common2, xT=xT[i], pos_rep=pos_rep[i], pos_tm=pos_tm[i], coef=K["coef"][i])
                    for i in range(NCORES)])
    if dbg is not None:
        dbg["xmix0"] = np.concatenate([r2[i]["xmix"].T for i in range(NCORES)], axis=0)
        dbg["x_l0"] = np.concatenate([r2[i]["x_out"].T for i in range(NCORES)], axis=0)
        dbg["k"] = np.concatenate([r2[i]["k_out"] for i in range(NCORES)], axis=0)
        dbg["v"] = np.concatenate([r2[i]["v_out"] for i in range(NCORES)], axis=0)
        if _debug == 2:
            return dbg
    bf = ml_dtypes.bfloat16
    Kf = np.concatenate([np.zeros((T, 1536), bf)] + [np.asarray(r2[i]["k_out"]) for i in range(NCORES)], axis=0)
    Vf = np.concatenate([np.zeros((T, 1536), bf)] + [np.asarray(r2[i]["v_out"]) for i in range(NCORES)], axis=0)
    valid = np.concatenate([np.zeros((T,), bf), np.ones((S,), bf)])
    kt_in = [[None] * 3 for _ in range(NCORES)]
    va_in = [[None] * 3 for _ in range(NCORES)]
    for ci in range(NCORES):
        base = T + ci * T
        for gi, (win, dil) in enumerate(DIL):
            nblk = T // (128 * dil)
            r_ = np.arange(dil)[:, None, None]
            kb_ = np.arange(nblk + 1)[None, :, None]
            n_ = np.arange(128)[None, None, :]
            tok = base - 128 * dil + (kb_ * 128 + n_) * dil + r_
            Kg = Kf[tok][..., gi * 512:(gi + 1) * 512]
            Kg = Kg.reshape(dil * (nblk + 1) * 128, 4, 128)
            kt_in[ci][gi] = A(np.transpose(Kg, (2, 1, 0)).reshape(128, -1))
            Vg = Vf[tok][..., gi * 512:(gi + 1) * 512].reshape(dil * (nblk + 1), 128, 4, 128)
            vl = valid[tok].reshape(dil * (nblk + 1), 128, 1, 1)
            Va = np.concatenate([Vg, np.broadcast_to(vl, Vg.shape[:3] + (1,))], axis=-1)
            va_in[ci][gi] = A(np.transpose(Va, (1, 0, 2, 3)).reshape(128, -1))
    nc3 = build(3, debug_mix=_debug is not None)
    common3 = dict(ident=K["ident"], c=c, invf16=K["invf16"], amask=K["amask"], aw=A(ada_w[1]), ab=row(ada_b[1]),
                   g0=row(norm_g[1, 0]), g1=row(norm_g[1, 1]), gfin=row(final_norm_g),
                   attn_w_q=A(attn_w_q[0]), attn_w_out=A(attn_w_out[0]), ffn_w_in=A(ffn_w_in[1]),
                   ffn_w_out=A(ffn_w_out[1]))
    in3 = []
    for i in range(NCORES):
        m = dict(common3, xT=A(r2[i]["x_out"]), pos_tm=pos_tm[i])
        for gi in range(3):
            m["ktg%d" % gi] = kt_in[i][gi]
            m["vag%d" % gi] = va_in[i][gi]
        in3.append(m)
    r3 = _run(nc3, in3)
    out = np.concatenate([r3[i]["x_out"].T for i in range(NCORES)], axis=0)[None].astype(f32)
    if dbg is not None:
        dbg["xmix1"] = np.concatenate([r3[i]["xmix"].T for i in range(NCORES)], axis=0)
        dbg["out"] = out
        return dbg
    return out
```
